# Optimizing a Trainium2 kernel written in Bass

```python
import jax, jax.numpy as jnp
from jax import lax
import numpy as np

D_MODEL = 1024
BATCH = 2
SEQ = 16384
DEPTH = 1
DEC_BATCH = 8
DEC_SEQ = 32
PAST_LEN = 4096

CHUNK = 64
D_MIX = D_MODEL
W_A = D_MIX // 2
W_B = D_MIX // 2
H_A = 8
HD_A = W_A // H_A
H_B = 8
HD_B = W_B // H_B
GMLP_CHUNK = 128
CONV_W = 4
LRU_C = 8.0
D_FF = 4 * D_MODEL
EPS = 1e-6

kernel_name = "hymba_gmlp_rglru_streaming_step"


def _rmsnorm(x, g):
    xf = x.astype(jnp.float32)
    y = xf * lax.rsqrt(jnp.mean(xf * xf, axis=-1, keepdims=True) + EPS)
    return (y * g.astype(jnp.float32)).astype(x.dtype)


def _layernorm(x, g, b):
    xf = x.astype(jnp.float32)
    mu = jnp.mean(xf, axis=-1, keepdims=True)
    var = jnp.mean(jnp.square(xf - mu), axis=-1, keepdims=True)
    y = (xf - mu) * lax.rsqrt(var + EPS)
    return (y * g.astype(jnp.float32) + b.astype(jnp.float32)).astype(x.dtype)


def _spatial_mix(v, w_s, b_s):
    B, T = v.shape[:2]
    w = jnp.tril(w_s)
    if T % GMLP_CHUNK == 0:
        n = T // GMLP_CHUNK
        vc = v.reshape(B, n, GMLP_CHUNK, H_A, HD_A)
        mixed = jnp.einsum('hts,bnshd->bnthd', w, vc) + b_s.T[:, :, None]
        return mixed.reshape(B, T, H_A, HD_A)
    wt = w[:, :T, :T]
    return jnp.einsum('hts,bshd->bthd', wt, v) + b_s[:, :T].T[:, :, None]


def _causal_conv(x, buf, w, b):
    T = x.shape[1]
    xp = jnp.concatenate([buf.astype(x.dtype), x], axis=1)
    y = b
    for k in range(CONV_W):
        y = y + xp[:, k:k + T] * w[k]
    return y, xp[:, -(CONV_W - 1):]


def _rg_lru(x, h0, w_a, b_a, w_x, b_x, lam, reset_first):
    B, T, C = x.shape
    xf = x.astype(jnp.float32)
    xh = xf.reshape(B, T, H_B, HD_B)
    r = jax.nn.sigmoid(jnp.einsum('bthi,hij->bthj', xh, w_a.astype(jnp.float32)).reshape(B, T, C) + b_a.astype(jnp.float32))
    i = jax.nn.sigmoid(jnp.einsum('bthi,hij->bthj', xh, w_x.astype(jnp.float32)).reshape(B, T, C) + b_x.astype(jnp.float32))
    log_a = -LRU_C * r * jax.nn.softplus(-lam.astype(jnp.float32))
    a = jnp.exp(log_a)
    mult = jnp.sqrt(-jnp.expm1(2.0 * log_a))
    if reset_first:
        mult = mult.at[:, 0].set(1.0)
    bterm = mult * (i * xf)
    bterm = bterm.at[:, 0].add(a[:, 0] * h0.astype(jnp.float32))

    def combine(l, rr):
        a1, b1 = l
        a2, b2 = rr
        return a1 * a2, a2 * b1 + b2

    _, h = lax.associative_scan(combine, (a, bterm), axis=1)
    return h.astype(x.dtype), h[:, -1].astype(x.dtype)


def _layer(x, conv_buf, h0, reset_first, norm1_g, w_in, ln_v_g, ln_v_b, w_s, b_s, conv_w, conv_b,
           w_a, b_a, w_x, b_x, lam, gn_a_g, gn_b_g, w_out, norm2_g, w_up, w_down):
    B, T, _ = x.shape
    xn = _rmsnorm(x, norm1_g)
    z = xn @ w_in
    u_a = jax.nn.gelu(z[..., :W_A])
    v_a = _layernorm(jax.nn.gelu(z[..., W_A:2 * W_A]), ln_v_g, ln_v_b)
    x_b = z[..., 2 * W_A:2 * W_A + W_B]
    g_b = z[..., 2 * W_A + W_B:]
    mixed = _spatial_mix(v_a.reshape(B, T, H_A, HD_A), w_s, b_s).reshape(B, T, W_A)
    y_a = u_a * mixed
    xc, conv_tail = _causal_conv(x_b, conv_buf, conv_w, conv_b)
    hs, h_last = _rg_lru(xc, h0, w_a, b_a, w_x, b_x, lam, reset_first)
    y_b = hs * jax.nn.gelu(g_b)
    o = jnp.concatenate([_rmsnorm(y_a, gn_a_g), _rmsnorm(y_b, gn_b_g)], axis=-1) @ w_out
    h = x + o
    hn = _rmsnorm(h, norm2_g)
    h = h + jnp.square(jax.nn.relu(hn @ w_up)) @ w_down
    return h, conv_tail, h_last, v_a


def setup_inputs(seed: int = 0) -> dict:
    key = jax.random.key(seed)
    ks = jax.random.split(key, 24)
    f = jnp.float32
    nrm = lambda k, shp, s: (jax.random.normal(k, shp, f) * s)
    a_c = jax.random.uniform(ks[13], (DEPTH, W_B), f, 0.9, 0.999)
    a0 = a_c ** (1.0 / LRU_C)
    lam = jnp.log(a0) - jnp.log1p(-a0)
    return {
        "x_prompt": nrm(ks[0], (BATCH, SEQ, D_MODEL), 1.0),
        "x_sample": nrm(ks[1], (DEC_BATCH, DEC_SEQ, D_MODEL), 1.0),
        "state_conv_b": nrm(ks[2], (DEPTH, DEC_BATCH, CONV_W - 1, W_B), 1.0),
        "state_h_b": nrm(ks[3], (DEPTH, DEC_BATCH, W_B), 0.5),
        "norm1_g": 1.0 + nrm(ks[4], (DEPTH, D_MODEL), 0.02),
        "w_in": nrm(ks[5], (DEPTH, D_MODEL, 2 * D_MIX), D_MODEL ** -0.5),
        "ln_v_g": 1.0 + nrm(ks[6], (DEPTH, W_A), 0.02),
        "ln_v_b": nrm(ks[7], (DEPTH, W_A), 0.02),
        "w_s": nrm(ks[8], (DEPTH, H_A, GMLP_CHUNK, GMLP_CHUNK), GMLP_CHUNK ** -0.5),
        "b_s": 1.0 + nrm(ks[9], (DEPTH, H_A, GMLP_CHUNK), 0.1),
        "conv_w": nrm(ks[10], (DEPTH, CONV_W, W_B), CONV_W ** -0.5),
        "conv_b": nrm(ks[11], (DEPTH, W_B), 0.02),
        "w_a": nrm(ks[12], (DEPTH, H_B, HD_B, HD_B), HD_B ** -0.5),
        "b_a": nrm(ks[14], (DEPTH, W_B), 0.02),
        "w_x": nrm(ks[15], (DEPTH, H_B, HD_B, HD_B), HD_B ** -0.5),
        "b_x": nrm(ks[16], (DEPTH, W_B), 0.02),
        "lam": lam,
        "gn_a_g": 1.0 + nrm(ks[17], (DEPTH, W_A), 0.02),
        "gn_b_g": 1.0 + nrm(ks[18], (DEPTH, W_B), 0.02),
        "w_out": nrm(ks[19], (DEPTH, D_MIX, D_MODEL), D_MIX ** -0.5),
        "norm2_g": 1.0 + nrm(ks[20], (DEPTH, D_MODEL), 0.02),
        "w_up": nrm(ks[21], (DEPTH, D_MODEL, D_FF), D_MODEL ** -0.5),
        "w_down": nrm(ks[22], (DEPTH, D_FF, D_MODEL), D_FF ** -0.5),
        "normf_g": 1.0 + nrm(ks[23], (D_MODEL,), 0.02),
    }


def reference(x_prompt, x_sample, state_conv_b, state_h_b, norm1_g, w_in, ln_v_g, ln_v_b, w_s, b_s,
              conv_w, conv_b, w_a, b_a, w_x, b_x, lam, gn_a_g, gn_b_g, w_out, norm2_g, w_up, w_down, normf_g):
    hp = x_prompt
    hs = x_sample
    conv_p, hlast_p, conv_s, hlast_s, v_s = [], [], [], [], []
    for l in range(DEPTH):
        p = (norm1_g[l], w_in[l], ln_v_g[l], ln_v_b[l], w_s[l], b_s[l], conv_w[l], conv_b[l],
             w_a[l], b_a[l], w_x[l], b_x[l], lam[l], gn_a_g[l], gn_b_g[l], w_out[l],
             norm2_g[l], w_up[l], w_down[l])
        zb = jnp.zeros((hp.shape[0], CONV_W - 1, W_B), hp.dtype)
        zh = jnp.zeros((hp.shape[0], W_B), hp.dtype)
        hp, cp, lp, _ = _layer(hp, zb, zh, True, *p)
        hs, cs, ls, vs = _layer(hs, state_conv_b[l], state_h_b[l], False, *p)
        conv_p.append(cp); hlast_p.append(lp)
        conv_s.append(cs); hlast_s.append(ls); v_s.append(vs)
    y_prompt = _rmsnorm(hp, normf_g)
    y_sample = _rmsnorm(hs, normf_g)
    return (y_prompt, y_sample, jnp.stack(conv_p), jnp.stack(hlast_p), jnp.stack(conv_s),
            jnp.stack(hlast_s), jnp.stack(v_s))
```

```python
import numpy as np
import concourse.bass as bass
import concourse.mybir as mybir
from concourse.bass_utils import run_bass_kernel_spmd
from contextlib import ExitStack

F32 = mybir.dt.float32
BF16 = mybir.dt.bfloat16
AF = mybir.ActivationFunctionType
ALU = mybir.AluOpType

D = 1024
TB = 512
NPRE = 24
NOWN = 8
NBLK = NPRE + NOWN
TS = 32
EPS = 1e-6
SAME_DIST = 10 ** 9


class Prog:
    ENGS = ["pe", "act", "dve", "pool", "sp"]
    LAT = 1.0
    TABLE_US = 1.3

    def __init__(self, nc, es, rings):
        self.nc = nc
        self.ops = []
        self.last_w = {}
        self.readers = {}
        self.eng_ops = {e: [] for e in self.ENGS}
        self.R = rings
        self.csem = {e: es.enter_context(nc.semaphore("c_" + e)) for e in ["pe", "act", "dve", "pool"]}
        self.dsem = {e: [es.enter_context(nc.semaphore("d_%s%d" % (e, i))) for i in range(rings[e])]
                     for e in rings}

    def op(self, eng, fn, reads=(), writes=(), dma=False, dur=0.5, table=None, issue=0.1):
        i = len(self.ops)
        deps = set()
        for r in reads:
            if r in self.last_w:
                deps.add(self.last_w[r])
        for w in writes:
            if w in self.last_w:
                deps.add(self.last_w[w])
            deps.update(self.readers.get(w, ()))
        for r in reads:
            self.readers.setdefault(r, []).append(i)
        for w in writes:
            self.last_w[w] = i
            self.readers[w] = []
        deps.discard(i)
        self.ops.append(dict(eng=eng, fn=fn, deps=deps, dma=dma, signal=dma, dur=dur, table=table, issue=issue,
                             tag=(getattr(self, "blk", -1), getattr(self, "cur_tag", ""))))
        self.eng_ops[eng].append(i)
        return i

    def dma(self, eng, fn, reads=(), writes=(), dur=3.0, issue=0.1):
        return self.op(eng, fn, reads, writes, dma=True, dur=dur, issue=issue)

    def schedule(self):
        import heapq
        ops = self.ops
        n = len(ops)
        succ = [[] for _ in range(n)]
        indeg = [0] * n
        for i, o in enumerate(ops):
            indeg[i] = len(o["deps"])
            for d in o["deps"]:
                succ[d].append(i)
        bl = [0.0] * n
        for i in range(n - 1, -1, -1):
            m = 0.0
            for s_ in succ[i]:
                if bl[s_] > m:
                    m = bl[s_]
            bl[i] = m + ops[i]["dur"] + (self.LAT if succ[i] else 0.0)
        finish = [0.0] * n
        rt = [0.0] * n
        cand = {e: [] for e in self.ENGS}
        pend = {e: [] for e in self.ENGS}
        for i in range(n):
            if indeg[i] == 0:
                heapq.heappush(pend[ops[i]["eng"]], (0.0, i))
        free = {e: 0.0 for e in self.ENGS}
        order = {e: [] for e in self.ENGS}
        recent = {e: [] for e in self.ENGS}
        cur_table = [None]
        bus_free = 0.0
        left = n
        while left:
            best_e, best_t = None, None
            for e in self.ENGS:
                if cand[e]:
                    t = free[e]
                elif pend[e]:
                    t = max(free[e], pend[e][0][0])
                else:
                    continue
                if best_t is None or t < best_t:
                    best_e, best_t = e, t
            e, t = best_e, best_t
            while pend[e] and pend[e][0][0] <= t:
                r_, i_ = heapq.heappop(pend[e])
                heapq.heappush(cand[e], (-bl[i_], i_))
            pick = heapq.heappop(cand[e])
            if e != "pe" and cand[e]:
                rec = recent[e]

                def hazard(ix):
                    for d_ in ops[ix]["deps"]:
                        if d_ in rec:
                            return True
                    return False
                if hazard(pick[1]):
                    alt = []
                    found = None
                    while cand[e] and len(alt) < 16:
                        c = heapq.heappop(cand[e])
                        if not hazard(c[1]) and (-c[0]) > (-pick[0]) - 60.0:
                            found = c
                            break
                        alt.append(c)
                    for c in alt:
                        heapq.heappush(cand[e], c)
                    if found is not None:
                        heapq.heappush(cand[e], pick)
                        pick = found
            if e == "act" and len(cand[e]) > 0:
                def needs_switch(ix):
                    tb = ops[ix]["table"]
                    if tb is None:
                        return False
                    if tb == "tanh":
                        return cur_table[0] not in ("exp", "gelu")
                    return tb != cur_table[0]
                if needs_switch(pick[1]):
                    alt = []
                    found = None
                    while cand[e] and len(alt) < 24:
                        c = heapq.heappop(cand[e])
                        if not needs_switch(c[1]) and (-c[0]) > (-pick[0]) - 40.0:
                            found = c
                            break
                        alt.append(c)
                    for c in alt:
                        heapq.heappush(cand[e], c)
                    if found is not None:
                        heapq.heappush(cand[e], pick)
                        pick = found
            i = pick[1]
            o = ops[i]
            start = t
            if e != "pe":
                for d_ in o["deps"]:
                    if d_ in recent[e]:
                        start += 0.4
                        break
            if e == "act" and o["table"] is not None:
                tb = o["table"]
                if tb == "tanh":
                    if cur_table[0] not in ("exp", "gelu"):
                        start += self.TABLE_US
                        cur_table[0] = "exp"
                elif tb != cur_table[0]:
                    start += self.TABLE_US
                    cur_table[0] = tb
            if o["dma"]:
                free[e] = start + o["issue"]
                xs_ = max(start + o["issue"], bus_free)
                bus_free = xs_ + max(o["dur"] - 2.0, 0.1)
                finish[i] = xs_ + o["dur"]
            else:
                finish[i] = start + o["dur"]
                free[e] = finish[i]
            order[e].append(i)
            o["t0"] = start
            o["t1"] = finish[i]
            recent[e] = (recent[e] + [i])[-2:]
            left -= 1
            for s_ in succ[i]:
                lat = 0.0 if (ops[s_]["eng"] == e and not o["dma"]) else self.LAT
                v = finish[i] + lat
                if v > rt[s_]:
                    rt[s_] = v
                indeg[s_] -= 1
                if indeg[s_] == 0:
                    heapq.heappush(pend[ops[s_]["eng"]], (rt[s_], s_))
        self.eng_ops = order
        self.sim_time = max(finish) if n else 0.0
        return self.sim_time

    def finalize(self):
        ops = self.ops
        for e in self.ENGS:
            for k, i in enumerate(self.eng_ops[e]):
                ops[i]["lidx"] = k
        for o in ops:
            need = {}
            dd = []
            for d in o["deps"]:
                p = ops[d]
                if p["dma"]:
                    dd.append(d)
                    continue
                if p["eng"] == o["eng"]:
                    if o["eng"] == "pe":
                        continue
                    if o["lidx"] - p["lidx"] > SAME_DIST:
                        continue
                if p["eng"] not in need or ops[need[p["eng"]]]["lidx"] < p["lidx"]:
                    need[p["eng"]] = d
            o["cw"] = list(need.values())
            o["dw"] = dd
            for d in o["cw"]:
                ops[d]["signal"] = True
        for e in self.ENGS:
            c = 0
            n = 0
            for i in self.eng_ops[e]:
                o = ops[i]
                if o["dma"]:
                    o["slot"] = n % self.R[e]
                    o["val"] = 16 * (n // self.R[e] + 1)
                    o["sem"] = self.dsem[e][o["slot"]]
                    n += 1
                elif o["signal"]:
                    c += 1
                    o["val"] = c
                    o["sem"] = self.csem[e]

        def emit(e, eng):
            waited = {}

            def wait(sem, val):
                k = id(sem)
                if waited.get(k, 0) >= val:
                    return
                waited[k] = val
                eng.wait_ge(sem, val)

            last = {}
            for i in self.eng_ops[e]:
                o = ops[i]
                for d in o["cw"] + o["dw"]:
                    wait(ops[d]["sem"], ops[d]["val"])
                if o["dma"]:
                    if o["val"] > 16:
                        wait(o["sem"], o["val"] - 16)
                    ins = o["fn"](eng)
                    ins.then_inc(o["sem"], 16)
                    last[o["slot"]] = o
                else:
                    ins = o["fn"](eng)
                    if o["signal"]:
                        ins.then_inc(o["sem"], 1)
            for o in last.values():
                wait(o["sem"], o["val"])

        with self.nc.Block() as block:
            @block.tensor
            def _(eng):
                emit("pe", eng)

            @block.scalar
            def _(eng):
                emit("act", eng)

            @block.vector
            def _(eng):
                emit("dve", eng)

            @block.gpsimd
            def _(eng):
                emit("pool", eng)

            @block.sync
            def _(eng):
                emit("sp", eng)


def build_nc():
    nc = bass.Bass("TRN2", target_bir_lowering=False)

    def din(name, shape):
        return nc.dram_tensor(name, list(shape), F32, kind="ExternalInput").ap()

    def dout(name, shape):
        return nc.dram_tensor(name, list(shape), F32, kind="ExternalOutput").ap()

    xs = din("xs", [NBLK * TB, D])
    xsm = din("xsm", [TS, D])
    sconv = din("sconv", [3, 512])
    sh = din("sh", [512])
    flags = din("flags", [96])
    c_ident = din("c_ident", [128, 128])
    c_tril = din("c_tril", [128, 128])
    c_E = din("c_E", [64, 512])
    norm1_g = din("norm1_g", [D])
    w_in = din("w_in", [D, 2048])
    ln_v_g = din("ln_v_g", [512])
    ln_v_b = din("ln_v_b", [512])
    w_s = din("w_s", [8, 128, 128])
    b_s = din("b_s", [8, 128])
    conv_w = din("conv_w", [4, 512])
    conv_b = din("conv_b", [512])
    w_a = din("w_a", [8, 64, 64])
    b_a = din("b_a", [512])
    w_x = din("w_x", [8, 64, 64])
    b_x = din("b_x", [512])
    lam = din("lam", [512])
    gn_a_g = din("gn_a_g", [512])
    gn_b_g = din("gn_b_g", [512])
    w_out = din("w_out", [D, D])
    norm2_g = din("norm2_g", [D])
    w_up = din("w_up", [D, 4096])
    w_down = din("w_down", [4096, D])
    normf_g = din("normf_g", [D])

    y = dout("y", [NOWN * TB, D])
    ysm = dout("ysm", [TS, D])
    oconv_p = dout("oconv_p", [3, 512])
    oh_p = dout("oh_p", [512])
    oconv_s = dout("oconv_s", [3, 512])
    oh_s = dout("oh_s", [512])
    ov_s = dout("ov_s", [TS, 512])

    scr_up = nc.dram_tensor("scr_up", [D, 4096], BF16, kind="Internal").ap()
    scr_dn = nc.dram_tensor("scr_dn", [4096, D], BF16, kind="Internal").ap()

    with ExitStack() as es:
        def sb(name, shape, dt=F32):
            return es.enter_context(nc.sbuf_tensor(name, list(shape), dt))

        def pt(name, shape, dt=F32):
            return es.enter_context(nc.psum_tensor(name, list(shape), dt))

        P = Prog(nc, es, rings={"sp": 12, "pool": 40})

        Win = sb("Win", [128, 8, 2048], BF16)
        Wout = sb("Wout", [128, 8, 1024], BF16)
        WsT = sb("WsT", [128, 8, 128], BF16)
        identb = sb("identb", [128, 128], BF16)
        onesb = sb("onesb", [128, 128], BF16)
        Eb = sb("Eb", [64, 512], BF16)
        bsT = sb("bsT", [64, 128], BF16)
        g2b = sb("g2b", [128, D])
        gfb = sb("gfb", [128, D])
        lngb = sb("lngb", [128, 512])
        lnbb = sb("lnbb", [128, 512])
        flg = sb("flg", [128, 96])
        cwh = sb("cwh", [128, 4, 4])
        fmv = sb("fmv", [128, 8, 4])
        g1fm = sb("g1fm", [128, 8])
        gnafm = sb("gnafm", [128, 4])
        Hst = sb("Hst", [128, 4])
        h0t = sb("h0t", [128, 4])
        XB = sb("XB", [128, 4, TB + 3])
        X = [sb("X0", [128, 4, D]), sb("X1", [128, 4, D])]
        xnb0 = sb("xnb0", [128, D], BF16)
        xnb = [xnb0, xnb0]
        xnb1 = sb("xnb1", [128, D], BF16)
        xnT = [sb("xnT0", [128, 8, TB], BF16), sb("xnT1", [128, 8, TB], BF16)]
        yT = sb("yT", [128, 8, TB], BF16)
        U = sb("U", [128, 512])
        VG = sb("VG", [128, 512])
        YA = sb("YA", [128, 512])
        vbf = sb("vbf", [128, 512], BF16)
        yabf = sb("yabf", [128, 512], BF16)
        Rtt = sb("Rtt", [128, 2, 512], BF16)
        Rt = [Rtt[:, 0, :], Rtt[:, 1, :]]
        WA32 = sb("WA32", [128, 4, 128])
        WX32 = sb("WX32", [128, 4, 128])
        xcb = sb("xcb", [128, 4, 512], BF16)
        stat = sb("stat", [128, 16, 8])
        bnst = sb("bnst", [128, 6])
        arena = sb("arena", [128, 32, TB], BF16)
        ring = [sb("ring%d" % i, [128, 2048], BF16) for i in range(3)]
        bsc = sb("bsc", [128, 6, 512])
        stg = X[1][:, 0, :].rearrange("p (a b) -> p a b", a=8)
        STG = "X1t0"

        def AU(u):
            return arena[:, 2 * u:2 * u + 2, :].bitcast(F32).rearrange("p a b -> p (a b)")

        def AUn(u):
            return "ar%d" % u

        ex15 = Rtt[:].bitcast(F32).rearrange("p a b -> p (a b)")
        cst = sb("cst", [128, 2])

        arena_u = arena[:].bitcast(F32).rearrange("p (u a) b -> p u (a b)", a=2)

        def PHYS(k):
            return (k % 4) * 4 + (k // 4)

        class SC0:
            @staticmethod
            def u(k):
                return arena_u[:, PHYS(k), :]

            @staticmethod
            def n(k):
                return "ar%d" % PHYS(k)

            @staticmethod
            def group(k0, T):
                kind = k0 // 4
                v = arena[:].bitcast(F32).rearrange("p (f k a) b -> p f k (a b)", k=4, a=2)
                return v[:, :, kind, :T], ["ar%d" % PHYS(k0 + i) for i in range(4)]

            @staticmethod
            def sqrt_groups(T):
                return [SC0.group(4, T)]

            @staticmethod
            def half_bf(k):
                return arena[:, 2 * PHYS(k), :]

        class SC1:
            @staticmethod
            def u(k):
                if k < 6:
                    return ring[k // 2][:, 1024 * (k % 2):1024 * (k % 2 + 1)].bitcast(F32)
                if k < 8:
                    return bsc[:, k - 6, :]
                if k < 12:
                    kk = k - 8
                    return yT[:, 2 * kk:2 * kk + 2, :].bitcast(F32).rearrange("p a b -> p (a b)")
                return [U[:], VG[:], YA[:], ex15][k - 12]

            @staticmethod
            def n(k):
                return "s1u%d" % k

            @staticmethod
            def group(k0, T):
                return None

            @staticmethod
            def sqrt_groups(T):
                v = ring[2][:].bitcast(F32).rearrange("p (a b) -> p a b", a=2)
                return [(v[:, :, :T], ["s1u4", "s1u5"]), (bsc[:, 0:2, :T], ["s1u6", "s1u7"])]

        class SCF:
            @staticmethod
            def u(kind, fc):
                return bsc[:, 3 * (fc % 2) + kind, :]

            @staticmethod
            def n(kind, fc):
                return "bs%d" % (3 * (fc % 2) + kind)

            @staticmethod
            def pair(kind, T):
                v = bsc[:].rearrange("p (f k) n -> p f k n", k=3)
                return v[:, :, kind, :T], ["bs%d" % kind, "bs%d" % (3 + kind)]

        pstb = [pt("pst0", [128, 1024], BF16)]
        pbank = [pt("pb%d" % i, [128, 512], F32) for i in range(7)]
        state = {"pb": 0, "ring": 0, "mb": 0}
        MIX_BANKS = [0, 1, 2]
        MLP_BANKS = [3, 4, 5, 6]

        def nbank():
            k = MIX_BANKS[state["pb"] % 3]
            state["pb"] += 1
            return pbank[k], "pb%d" % k

        def mbank():
            k = MLP_BANKS[state["mb"] % 4]
            state["mb"] += 1
            return pbank[k], "pb%d" % k

        def npst():
            return pstb[0], "pst0"

        def nring():
            k = state["ring"]
            state["ring"] = (k + 1) % 3
            return ring[k], "ring%d" % k

        def fsz(ap):
            n = 1
            for d_ in ap.shape[1:]:
                n *= d_
            return n

        TABLE = {AF.Exp: "exp", AF.Tanh: "tanh", AF.Gelu_apprx_tanh: "gelu", AF.Sqrt: "sqrt", AF.Ln: "ln"}

        def act(out, in_, func, reads, writes, **kw):
            P.op("act", lambda e: e.activation(out=out, in_=in_, func=func, **kw), reads, writes,
                 dur=0.2 + fsz(in_) / 1000.0, table=TABLE.get(func))

        def edur(eng, n):
            if eng == "pool":
                return 0.2 + n / 600.0
            return 0.15 + n / 850.0

        def ts(eng, out, in0, s1, s2, op0, op1, reads, writes):
            if s2 is None:
                P.op(eng, lambda e: e.tensor_scalar(out=out, in0=in0, scalar1=s1, scalar2=None, op0=op0), reads, writes,
                     dur=edur(eng, fsz(out)))
            else:
                P.op(eng, lambda e: e.tensor_scalar(out=out, in0=in0, scalar1=s1, scalar2=s2, op0=op0, op1=op1), reads, writes,
                     dur=edur(eng, fsz(out)))

        def tt(eng, out, in0, in1, op, reads, writes):
            P.op(eng, lambda e: e.tensor_tensor(out=out, in0=in0, in1=in1, op=op), reads, writes,
                 dur=(0.2 + fsz(out) / 500.0) if eng == "pool" else edur(eng, fsz(out)))

        def stt(out, in0, scalar, in1, op0, op1, reads, writes):
            P.op("dve", lambda e: e.scalar_tensor_tensor(out=out, in0=in0, scalar=scalar, in1=in1, op0=op0, op1=op1), reads, writes,
                 dur=0.15 + fsz(out) / 850.0)

        def scan(out, d0, d1, init, reads, writes):
            P.op("dve", lambda e: e.tensor_tensor_scan(out=out, data0=d0, data1=d1, initial=init, op0=ALU.mult, op1=ALU.add),
                 reads, writes, dur=0.2 + fsz(out) / 480.0)

        def recip(out, in_, reads, writes):
            P.op("dve", lambda e: e.reciprocal(out=out, in_=in_), reads, writes, dur=0.15 + fsz(out) / 850.0)

        def cp(eng, out, in_, reads, writes):
            if eng == "act":
                act(out, in_, AF.Copy, reads, writes)
            else:
                n = fsz(out)
                d_ = (0.2 + n / 280.0) if eng == "pool" else (0.15 + n / 850.0)
                P.op(eng, lambda e: e.tensor_copy(out=out, in_=in_), reads, writes, dur=d_)

        def mm(out, lhsT, rhs, start, stop, reads, writes, f32=False):
            P.op("pe", lambda e: e.matmul(out, lhsT=lhsT, rhs=rhs, start=start, stop=stop), reads, writes,
                 dur=(0.02 + fsz(rhs) / 2300.0) * (4.0 if f32 else 1.0))

        def tr(out, in_, ident, reads, writes):
            P.op("pe", lambda e: e.transpose(out, in_, ident), reads, writes, dur=0.1)

        def ld(out, in_, writes, reads=(), eng="sp", slow=False, us=3.0, issue=0.1):
            if slow:
                P.dma(eng, lambda e: e.dma_start(out=out, in_=in_, allow_slow_non_contiguous=True), reads, writes, dur=us, issue=issue)
            else:
                P.dma(eng, lambda e: e.dma_start(out=out, in_=in_), reads, writes, dur=us, issue=issue)

        def memset(eng, ap, val, writes, reads=()):
            P.op(eng, lambda e: e.memset(ap, val), reads, writes, dur=0.2 + fsz(ap) / 1000.0)

        def ppow(out, in0, col, reads, writes):
            ex = cst[:out.shape[0], col:col + 1].to_broadcast(list(out.shape))
            P.op("pool", lambda e: e.tensor_tensor(out=out, in0=in0, in1=ex, op=ALU.pow), list(reads) + ["cst"], writes,
                 dur=0.3 + fsz(out) * 0.16)

        def rsqrt_small(dst, src, scale, reads, writes):
            ts("pool", dst, src, scale, EPS, ALU.mult, ALU.add, reads, writes)
            ppow(dst, dst, 0, writes, writes)

        memset("pool", cst[:, 0:1], -0.5, ["cst"])
        memset("pool", cst[:, 1:2], 0.5, ["cst"], reads=["cst"])
        memset("pool", onesb[:], 1.0, ["onesb"])
        memset("pool", bsT[:], 0.0, ["bsT"])
        memset("pool", Hst[:], 0.0, ["Hst"])
        memset("pool", XB[:], 0.0, ["XB0", "XB1", "XB2", "XB3"])
        for kc in range(8):
            ld(Win[:, kc, :], w_in[kc * 128:(kc + 1) * 128, :], ["Win%d" % kc], eng="pool", us=8.0, issue=8.0)
        deferred = []
        for kc in range(8):
            deferred.append((Wout[:, kc, :], w_out[kc * 128:(kc + 1) * 128, :], "Wout%d" % kc))
        for i in range(8):
            deferred.append((scr_up[i * 128:(i + 1) * 128, :], w_up[i * 128:(i + 1) * 128, :], "scr_up"))
        for i in range(8):
            deferred.append((scr_dn[i * 512:(i + 1) * 512, :], w_down[i * 512:(i + 1) * 512, :], "scr_dn"))

        ld(g2b[:], norm2_g.partition_broadcast(128), ["g2b"])
        ld(gfb[:], normf_g.partition_broadcast(128), ["gfb"])
        ld(lngb[:], ln_v_g.partition_broadcast(128), ["lngb"])
        ld(lnbb[:], ln_v_b.partition_broadcast(128), ["lnbb"])
        ld(flg[:], flags.partition_broadcast(128), ["flg"])
        ld(g1fm[:], norm1_g.rearrange("(c p) -> p c", p=128), ["g1fm"], slow=True)
        ld(gnafm[:], gn_a_g.rearrange("(c p) -> p c", p=128), ["gnafm"], slow=True)
        for kc in range(8):
            ts("dve", Win[:, kc, :], Win[:, kc, :], g1fm[:, kc:kc + 1], None, ALU.mult, None, ["Win%d" % kc, "g1fm"], ["Win%d" % kc])
        WIN = ["Win%d" % kc for kc in range(8)]
        WOUT = ["Wout%d" % kc for kc in range(8)]

        ld(stg[:, 0, :], c_ident, [STG])
        cp("dve", identb[:], stg[:, 0, :], [STG], ["identb"])
        ld(YA[0:64, :], c_E, ["YA"])
        cp("dve", Eb[:], YA[0:64, :], ["YA"], ["Eb"])
        ld(VG[0:8, 0:128], b_s, ["VG"])
        ld(VG[32:40, 128:256], b_s, ["VG"])
        cp("dve", bsT[0:8, :], VG[0:8, 0:128], ["VG"], ["bsT"])
        cp("dve", yabf[32:40, 0:128], VG[32:40, 128:256], ["VG"], ["yabf"])
        cp("dve", VG[32:40, 256:384], yabf[32:40, 0:128], ["yabf"], ["VG"])
        tt("dve", VG[32:40, 128:256], VG[32:40, 128:256], VG[32:40, 256:384], ALU.subtract, ["VG"], ["VG"])
        cp("dve", bsT[32:40, :], VG[32:40, 128:256], ["VG"], ["bsT"])

        ld(stg[:], w_s.rearrange("h t s -> t h s"), [STG], reads=[STG])
        ld(U[:, 0:128], c_tril, ["U"])
        for h in range(8):
            tt("dve", YA[:, 0:128], stg[:, h, :], U[:, 0:128], ALU.mult, [STG, "U"], ["YA"])
            cp("dve", vbf[:, 0:128], YA[:, 0:128], ["YA"], ["vbf"])
            pstt, pn = npst()
            tr(pstt[:, 0:128], vbf[:, 0:128], identb[:], ["vbf", "identb"], [pn])
            cp("dve", WsT[:, h, :], pstt[:, 0:128], [pn], ["WsT"])

        for (wsrc, w32, nm) in ((w_a, WA32, "WA"), (w_x, WX32, "WX")):
            memset("pool", w32[:], 0.0, [nm + "32"])
            for h in range(8):
                po = (h % 2) * 64
                ld(w32[po:po + 64, h // 2, po:po + 64], wsrc[h], [nm + "32"], reads=[nm + "32"])

        def fml(dst, src, nm):
            ld(dst, src.rearrange("(c p) -> p c", p=128), [nm], slow=True)
        for k in range(4):
            fml(cwh[:, k, :], conv_w[k], "cwh")
        fml(fmv[:, 0, :], conv_b, "fmv0")
        fml(fmv[:, 1, :], b_a, "fmv1")
        fml(fmv[:, 2, :], b_x, "fmv2")
        fml(fmv[:, 3, :], lam, "fmv3")
        fml(fmv[:, 4, :], gn_b_g, "fmv4")
        ts("dve", cwh[:].rearrange("p a b -> p (a b)"), cwh[:].rearrange("p a b -> p (a b)"), 0.5, None, ALU.mult, None, ["cwh"], ["cwh"])
        for i in range(3):
            ts("dve", fmv[:, i, :], fmv[:, i, :], 0.5, None, ALU.mult, None, ["fmv%d" % i], ["fmv%d" % i])
        act(fmv[:, 7, :], fmv[:, 3, :], AF.Exp, ["fmv3"], ["fmv7"], scale=-1.0)
        act(fmv[:, 7, :], fmv[:, 7, :], AF.Ln, ["fmv7"], ["fmv7"], bias=1.0)
        ts("dve", fmv[:, 5, :], fmv[:, 7, :], -4.0, None, ALU.mult, None, ["fmv7"], ["fmv5"])
        ts("dve", fmv[:, 6, :], fmv[:, 7, :], -8.0, None, ALU.mult, None, ["fmv7"], ["fmv6"])

        class Blk:
            pass

        def mkblk(j, kind):
            b = Blk()
            b.j = j
            b.sample = kind == "sample"
            b.full = kind != "prefix"
            b.T = TS if b.sample else TB
            b.PT = TS if b.sample else 128
            b.NT = 1 if b.sample else 4
            b.slot = 0 if b.sample else j % 2
            b.Xt = X[b.slot]
            b.Xn = ["X%dt%d" % (b.slot, t_) for t_ in range(b.NT)]
            b.xT = xnT[b.slot]
            b.xTn = "xnT%d" % b.slot
            b.sc = SC1 if (kind == "prefix" and j % 2 == 1) else SC0
            return b

        def st_load(b):
            if b.sample:
                ld(b.Xt[:b.PT, 0, :], xsm, b.Xn)
            else:
                ld(b.Xt[:], xs[b.j * TB:(b.j + 1) * TB, :].rearrange("(t p) d -> p t d", p=128), b.Xn, us=9.0)

        def st_norm(b, gb, gname):
            PT, NT = b.PT, b.NT
            for t_ in range(NT):
                act(xnb0[:PT, :], b.Xt[:PT, t_, :], AF.Square, [b.Xn[t_]], ["ssq", "xnb0"], accum_out=stat[:PT, 0, t_:t_ + 1])
            rsqrt_small(stat[:PT, 1, 0:NT], stat[:PT, 0, 0:NT], 1.0 / D, ["ssq"], ["rstd"])
            yield

            def scale(t_):
                xb_, xbn = xnb[t_ % 2], "xnb0"
                if gb is None:
                    ts("dve", xb_[:PT, :], b.Xt[:PT, t_, :], stat[:PT, 1, t_:t_ + 1], None, ALU.mult, None,
                       [b.Xn[t_], "rstd"], [xbn])
                else:
                    stt(xb_[:PT, :], b.Xt[:PT, t_, :], stat[:PT, 1, t_:t_ + 1], gb[:PT, :], ALU.mult, ALU.mult,
                        [b.Xn[t_], "rstd", gname], [xbn])

            def trans(t_):
                xb_, xbn = xnb[t_ % 2], "xnb0"
                if gb is not None:
                    mb_, pn = mbank()
                    pstt = mb_[:].bitcast(BF16)
                else:
                    pstt, pn = npst()
                for kc in range(8):
                    tr(pstt[:, kc * PT:(kc + 1) * PT], xb_[:PT, kc * 128:(kc + 1) * 128], identb[:PT, :PT],
                       [xbn, "identb"], [pn])
                eng = "act" if t_ % 2 == 0 else "dve"
                cp(eng, b.xT[:, :, t_ * 128:t_ * 128 + PT], pstt[:, 0:8 * PT].rearrange("p (k t) -> p k t", k=8),
                   [pn], [b.xTn])

            scale(0)
            yield
            for t_ in range(1, NT):
                trans(t_ - 1)
                scale(t_)
                yield
            trans(NT - 1)
            yield

        xstg = xcb[:].bitcast(F32).rearrange("p a b -> p (a b)")
        XSTG = ["xcb0", "xcb1", "xcb2", "xcb3"]

        def st_norm1s(b):
            for t_ in range(4):
                sq, rs = "ssq1_%d" % t_, "rstd1_%d" % t_
                ld(xstg, xs[b.j * TB + t_ * 128:b.j * TB + (t_ + 1) * 128, :], XSTG, us=3.5)
                act(xnb1[:, :], xstg, AF.Square, XSTG, [sq, "xnb1"], accum_out=stat[:, 9, t_:t_ + 1])
                ts("pool", stat[:, 10, t_:t_ + 1], stat[:, 9, t_:t_ + 1], 1.0 / D, EPS, ALU.mult, ALU.add, [sq], [rs])
                ppow(stat[:, 10, t_:t_ + 1], stat[:, 10, t_:t_ + 1], 0, [rs], [rs])
                ts("dve", xnb1[:, :], xstg, stat[:, 10, t_:t_ + 1], None, ALU.mult, None, XSTG + [rs], ["xnb1"])
                pstt, pn = npst()
                for kc in range(8):
                    tr(pstt[:, kc * 128:(kc + 1) * 128], xnb1[:, kc * 128:(kc + 1) * 128], identb[:, :],
                       ["xnb1", "identb"], [pn])
                eng = "act" if t_ % 2 == 0 else "dve"
                cp(eng, b.xT[:, :, t_ * 128:(t_ + 1) * 128], pstt[:, 0:1024].rearrange("p (k t) -> p k t", k=8),
                   [pn], [b.xTn])
                yield

        def st_b1(b):
            T, j = b.T, b.j
            AU, AUn = b.sc.u, b.sc.n
            merged = b.sc.group(4, T) is not None
            if b.sample:
                for fc in range(4):
                    ld(XB[:, fc, 0:3], sconv[:, fc * 128:(fc + 1) * 128].rearrange("k p -> p k"), ["XB%d" % fc], slow=True)
            pbs = []
            for fc in range(4):
                pb, pbn = nbank()
                for kc in range(8):
                    mm(pb[:, :T], Win[:, kc, 1024 + fc * 128:1024 + (fc + 1) * 128], b.xT[:, kc, :T], kc == 0, kc == 7,
                       [WIN[kc], b.xTn], [pbn])
                if not b.sample and j > 0:
                    cp("act", XB[:, fc, 0:3], XB[:, fc, TB:TB + 3], ["XB%d" % fc], ["XB%d" % fc])
                cp("act", XB[:, fc, 3:3 + T], pb[:, :T], [pbn], ["XB%d" % fc])
                yield
            for fc in range(4):
                TMP, TMPn = AU(12 + fc), AUn(12 + fc)
                ts("pool", TMP[:, :T], XB[:, fc, 0:T], cwh[:, 0, fc:fc + 1], fmv[:, 0, fc:fc + 1], ALU.mult, ALU.add,
                   ["XB%d" % fc, "cwh", "fmv0"], [TMPn])
            yield
            for k in range(1, 4):
                for fc in range(4):
                    TMP, TMPn = AU(12 + fc), AUn(12 + fc)
                    stt(TMP[:, :T], XB[:, fc, k:k + T], cwh[:, k, fc:fc + 1], TMP[:, :T], ALU.mult, ALU.add,
                        ["XB%d" % fc, "cwh", TMPn], [TMPn])
                yield
            f32g = not b.full
            if not f32g:
                for fc in range(4):
                    cp("pool", xcb[:, fc, :T], AU(12 + fc)[:, :T], [AUn(12 + fc)], ["xcb%d" % fc])
            yield
            for fc in range(4):
                A_, T1_, V_ = AU(fc), AU(4 + fc), AU(8 + fc)
                An, T1n, Vn = AUn(fc), AUn(4 + fc), AUn(8 + fc)
                pr, prn = nbank()
                if f32g:
                    mm(pr[:, :T], WA32[:, fc, :], AU(12 + fc)[:, :T], True, True, ["WA32", AUn(12 + fc)], [prn], f32=True)
                else:
                    mm(pr[:, :T], WA[:, fc, :], xcb[:, fc, :T], True, True, ["WA", "xcb%d" % fc], [prn])
                act(T1_[:, :T], pr[:, :T], AF.Tanh, [prn, "fmv1"], [T1n], bias=fmv[:, 1, fc:fc + 1])
                pi, pin = nbank()
                if f32g:
                    mm(pi[:, :T], WX32[:, fc, :], AU(12 + fc)[:, :T], True, True, ["WX32", AUn(12 + fc)], [pin], f32=True)
                else:
                    mm(pi[:, :T], WX[:, fc, :], xcb[:, fc, :T], True, True, ["WX", "xcb%d" % fc], [pin])
                act(V_[:, :T], pi[:, :T], AF.Tanh, [pin, "fmv2"], [Vn], bias=fmv[:, 2, fc:fc + 1])
                yield
            for fc in range(4):
                A_, T1_, V_ = AU(fc), AU(4 + fc), AU(8 + fc)
                An, T1n, Vn = AUn(fc), AUn(4 + fc), AUn(8 + fc)
                TMP, TMPn = AU(12 + fc), AUn(12 + fc)
                act(A_[:, :T], T1_[:, :T], AF.Exp, [T1n, "fmv5"], [An], scale=fmv[:, 5, fc:fc + 1], bias=fmv[:, 5, fc:fc + 1])
                if merged:
                    continue
                tt("pool", T1_[:, :T], A_[:, :T], A_[:, :T], ALU.mult, [An], [T1n])
                ts("pool", T1_[:, :T], T1_[:, :T], -1.0, 1.0, ALU.mult, ALU.add, [T1n], [T1n])
                stt(V_[:, :T], V_[:, :T], 1.0, TMP[:, :T], ALU.add, ALU.mult, [Vn, TMPn], [Vn])
                if (not b.sample) and j % 8 == 0:
                    ts("pool", T1_[:, 0:1], T1_[:, 0:1], flg[:, 64 + j:64 + j + 1], flg[:, 32 + j:32 + j + 1],
                       ALU.mult, ALU.add, [T1n, "flg"], [T1n])
                yield
            if merged:
                gT1, nT1 = b.sc.group(4, T)
                gV, nV = b.sc.group(8, T)
                gTM, nTM = b.sc.group(12, T)
                gA, nA = b.sc.group(0, T)
                tt("pool", gT1, gA, gA, ALU.mult, nA, nT1)
                ts("pool", gT1, gT1, -1.0, 1.0, ALU.mult, ALU.add, nT1, nT1)
                stt(gV, gV, 1.0, gTM, ALU.add, ALU.mult, nV + nTM, nV)
                if (not b.sample) and j % 8 == 0:
                    for fc in range(4):
                        T1_, T1n = AU(4 + fc), AUn(4 + fc)
                        ts("pool", T1_[:, 0:1], T1_[:, 0:1], flg[:, 64 + j:64 + j + 1], flg[:, 32 + j:32 + j + 1],
                           ALU.mult, ALU.add, [T1n, "flg"], [T1n])
                yield

        def st_b2(b):
            T, j = b.T, b.j
            AU, AUn = b.sc.u, b.sc.n
            if b.sample:
                ld(h0t[:], sh.rearrange("(c p) -> p c", p=128), ["h0t"], slow=True)
            else:
                ts("dve", h0t[:], Hst[:], flg[:, j:j + 1], None, ALU.mult, None, ["Hst", "flg"], ["h0t"])
            for (v_, n_) in b.sc.sqrt_groups(T):
                act(v_, v_, AF.Sqrt, n_, n_)
            yield
            if b.sc.group(4, T) is not None:
                gT1, nT1 = b.sc.group(4, T)
                gV, nV = b.sc.group(8, T)
                tt("dve", gV, gV, gT1, ALU.mult, nV + nT1, nV)
            else:
                for fc in range(4):
                    A_, T1_, V_ = AU(fc), AU(4 + fc), AU(8 + fc)
                    An, T1n, Vn = AUn(fc), AUn(4 + fc), AUn(8 + fc)
                    tt("dve", V_[:, :T], V_[:, :T], T1_[:, :T], ALU.mult, [Vn, T1n], [Vn])
            yield
            for fc in range(4):
                A_, T1_, V_ = AU(fc), AU(4 + fc), AU(8 + fc)
                An, T1n, Vn = AUn(fc), AUn(4 + fc), AUn(8 + fc)
                scan(T1_[:, :T], A_[:, :T], V_[:, :T], h0t[:, fc:fc + 1], [An, Vn, "h0t"], [T1n])
                cp("act", Hst[:, fc:fc + 1], T1_[:, T - 1:T], [T1n], ["Hst", "tok%d_%d" % (b.j, int(b.sample))])
                yield
            if not b.full:
                return
            for fc in range(4):
                A_, T1_, V_ = AU(fc), AU(4 + fc), AU(8 + fc)
                An, T1n, Vn = AUn(fc), AUn(4 + fc), AUn(8 + fc)
                TMP, TMPn = AU(12 + fc), AUn(12 + fc)
                pg, pgn = nbank()
                for kc in range(8):
                    mm(pg[:, :T], Win[:, kc, 1536 + fc * 128:1536 + (fc + 1) * 128], b.xT[:, kc, :T], kc == 0, kc == 7,
                       [WIN[kc], b.xTn], [pgn])
                act(TMP[:, :T], pg[:, :T], AF.Gelu_apprx_tanh, [pgn], [TMPn])
                stt(A_[:, :T], T1_[:, :T], fmv[:, 4, fc:fc + 1], TMP[:, :T], ALU.mult, ALU.mult, [T1n, "fmv4", TMPn], [An])
                ysq = SC0.half_bf(8 + fc)
                act(ysq[:, :T], A_[:, :T], AF.Square, [An], [Vn])
            yield
            pb, pbn = nbank()
            for fc in range(4):
                mm(pb[:, :T], onesb[:], SC0.half_bf(8 + fc)[:, :T], fc == 0, fc == 3, ["onesb", AUn(8 + fc)], [pbn])
            RB, RBn = AU(12), AUn(12)
            ts("dve", RB[:, :T], pb[:, :T], 1.0 / 512, EPS, ALU.mult, ALU.add, [pbn], [RBn])
            act(RB[:, :T], RB[:, :T], AF.Sqrt, [RBn], [RBn])
            recip(RB[:, :T], RB[:, :T], [RBn], [RBn])
            for fc in range(4):
                tt("pool", yT[:, 4 + fc, :T], AU(fc)[:, :T], RB[:, :T], ALU.mult, [AUn(fc), RBn], ["yTb"])
            yield

        def st_bf(b):
            T, j, PT, NT = b.T, b.j, b.PT, b.NT
            if b.sample:
                ld(h0t[:], sh.rearrange("(c p) -> p c", p=128), ["h0t"], slow=True)
                for fc in range(4):
                    ld(XB[:, fc, 0:3], sconv[:, fc * 128:(fc + 1) * 128].rearrange("k p -> p k"), ["XB%d" % fc], slow=True)
            else:
                ts("dve", h0t[:], Hst[:], flg[:, j:j + 1], None, ALU.mult, None, ["Hst", "flg"], ["h0t"])
            for half in range(2):
                fcs = (2 * half, 2 * half + 1)
                for fc in fcs:
                    pb, pbn = nbank()
                    for kc in range(8):
                        mm(pb[:, :T], Win[:, kc, 1024 + fc * 128:1024 + (fc + 1) * 128], b.xT[:, kc, :T], kc == 0, kc == 7,
                           [WIN[kc], b.xTn], [pbn])
                    if not b.sample and j > 0:
                        cp("act", XB[:, fc, 0:3], XB[:, fc, TB:TB + 3], ["XB%d" % fc], ["XB%d" % fc])
                    cp("act", XB[:, fc, 3:3 + T], pb[:, :T], [pbn], ["XB%d" % fc])
                yield
                for fc in fcs:
                    ts("pool", SCF.u(0, fc)[:, :T], XB[:, fc, 0:T], cwh[:, 0, fc:fc + 1], fmv[:, 0, fc:fc + 1], ALU.mult, ALU.add,
                       ["XB%d" % fc, "cwh", "fmv0"], [SCF.n(0, fc)])
                for k in range(1, 4):
                    for fc in fcs:
                        stt(SCF.u(0, fc)[:, :T], XB[:, fc, k:k + T], cwh[:, k, fc:fc + 1], SCF.u(0, fc)[:, :T], ALU.mult, ALU.add,
                            ["XB%d" % fc, "cwh", SCF.n(0, fc)], [SCF.n(0, fc)])
                yield
                for fc in fcs:
                    TM, TMn = SCF.u(0, fc), SCF.n(0, fc)
                    T1_, T1n = SCF.u(1, fc), SCF.n(1, fc)
                    V_, Vn = SCF.u(2, fc), SCF.n(2, fc)
                    pr, prn = nbank()
                    mm(pr[:, :T], WA32[:, fc, :], TM[:, :T], True, True, ["WA32", TMn], [prn], f32=True)
                    act(T1_[:, :T], pr[:, :T], AF.Tanh, [prn, "fmv1"], [T1n], bias=fmv[:, 1, fc:fc + 1])
                    pi, pin = nbank()
                    mm(pi[:, :T], WX32[:, fc, :], TM[:, :T], True, True, ["WX32", TMn], [pin], f32=True)
                    act(V_[:, :T], pi[:, :T], AF.Tanh, [pin, "fmv2"], [Vn], bias=fmv[:, 2, fc:fc + 1])
                    stt(V_[:, :T], V_[:, :T], 1.0, TM[:, :T], ALU.add, ALU.mult, [Vn, TMn], [Vn])
                    yield
                for fc in fcs:
                    A_, An = SCF.u(0, fc), SCF.n(0, fc)
                    T1_, T1n = SCF.u(1, fc), SCF.n(1, fc)
                    act(A_[:, :T], T1_[:, :T], AF.Exp, [T1n, "fmv5"], [An], scale=fmv[:, 5, fc:fc + 1], bias=fmv[:, 5, fc:fc + 1])
                    act(T1_[:, :T], T1_[:, :T], AF.Exp, [T1n, "fmv6"], [T1n], scale=fmv[:, 6, fc:fc + 1], bias=fmv[:, 6, fc:fc + 1])
                gT1, nT1 = SCF.pair(1, T)
                gV, nV = SCF.pair(2, T)
                ts("pool", gT1, gT1, -1.0, 1.0, ALU.mult, ALU.add, nT1, nT1)
                if (not b.sample) and j % 8 == 0:
                    for fc in fcs:
                        T1_, T1n = SCF.u(1, fc), SCF.n(1, fc)
                        ts("pool", T1_[:, 0:1], T1_[:, 0:1], flg[:, 64 + j:64 + j + 1], flg[:, 32 + j:32 + j + 1],
                           ALU.mult, ALU.add, [T1n, "flg"], [T1n])
                act(gT1, gT1, AF.Sqrt, nT1, nT1)
                tt("dve", gV, gV, gT1, ALU.mult, nV + nT1, nV)
                yield
                for fc in fcs:
                    A_, An = SCF.u(0, fc), SCF.n(0, fc)
                    T1_, T1n = SCF.u(1, fc), SCF.n(1, fc)
                    V_, Vn = SCF.u(2, fc), SCF.n(2, fc)
                    scan(T1_[:, :T], A_[:, :T], V_[:, :T], h0t[:, fc:fc + 1], [An, Vn, "h0t"], [T1n])
                    cp("act", Hst[:, fc:fc + 1], T1_[:, T - 1:T], [T1n], ["Hst", "tok%d_%d" % (b.j, int(b.sample))])
                    pg, pgn = nbank()
                    for kc in range(8):
                        mm(pg[:, :T], Win[:, kc, 1536 + fc * 128:1536 + (fc + 1) * 128], b.xT[:, kc, :T], kc == 0, kc == 7,
                           [WIN[kc], b.xTn], [pgn])
                    act(A_[:, :T], pg[:, :T], AF.Gelu_apprx_tanh, [pgn, An], [An])
                    stt(yT[:, 4 + fc, :T], T1_[:, :T], fmv[:, 4, fc:fc + 1], A_[:, :T], ALU.mult, ALU.mult, [T1n, "fmv4", An], ["yTb%d" % fc])
                    act(xcb[:, fc, :T], yT[:, 4 + fc, :T], AF.Square, ["yTb%d" % fc], ["xcb%d" % fc])
                    yield
            pk, pkn = nbank()
            for t_ in range(NT):
                tok = slice(t_ * 128, t_ * 128 + PT)
                for fc in range(4):
                    mm(pk[:PT, t_:t_ + 1], xcb[:, fc, tok], onesb[:, 0:1], fc == 0, fc == 3, ["xcb%d" % fc, "onesb"], [pkn])
            ts("dve", stat[:PT, 8, 0:NT], pk[:PT, 0:NT], 1.0 / 512, EPS, ALU.mult, ALU.add, [pkn], ["rsb"])
            ppow(stat[:PT, 8, 0:NT], stat[:PT, 8, 0:NT], 0, ["rsb"], ["rsb"])
            yield

        def st_a(b):
            PT, NT = b.PT, b.NT
            for t_ in range(NT):
                tok = slice(t_ * 128, t_ * 128 + PT)
                pu, pun = nbank()
                for kc in range(8):
                    mm(pu[:PT, :], b.xT[:, kc, tok], Win[:, kc, 0:512], kc == 0, kc == 7, [WIN[kc], b.xTn], [pun])
                pv, pvn = nbank()
                for kc in range(8):
                    mm(pv[:PT, :], b.xT[:, kc, tok], Win[:, kc, 512:1024], kc == 0, kc == 7, [WIN[kc], b.xTn], [pvn])
                act(U[:PT, :], pu[:PT, :], AF.Gelu_apprx_tanh, [pun], ["U"])
                act(VG[:PT, :], pv[:PT, :], AF.Gelu_apprx_tanh, [pvn], ["VG"])
                yield
                P.op("dve", lambda e: e.bn_stats(out=bnst[:PT, :], in_=VG[:PT, :]), ["VG"], ["bnst"], dur=0.75)
                P.op("dve", lambda e: e.bn_aggr(out=stat[:PT, 2, 0:2], in_=bnst[:PT, :]), ["bnst"], ["mv"], dur=0.2)
                rsqrt_small(stat[:PT, 3, 0:1], stat[:PT, 2, 1:2], 1.0, ["mv"], ["lnr"])
                stt(stat[:PT, 3, 1:2], stat[:PT, 2, 0:1], -1.0, stat[:PT, 3, 0:1], ALU.mult, ALU.mult, ["mv", "lnr"], ["lnb2"])
                act(VG[:PT, :], VG[:PT, :], AF.Identity, ["VG", "lnr", "lnb2"], ["VG"], scale=stat[:PT, 3, 0:1], bias=stat[:PT, 3, 1:2])
                tt("dve", VG[:PT, :], VG[:PT, :], lngb[:PT, :], ALU.mult, ["VG", "lngb"], ["VG"])
                if b.sample:
                    tt("pool", VG[:PT, :], VG[:PT, :], lnbb[:PT, :], ALU.add, ["VG", "lnbb"], ["VG"])
                    ld(ov_s, VG[:PT, :], [], reads=["VG"])
                    cp("pool", vbf[:PT, :], VG[:PT, :], ["VG"], ["vbf"])
                else:
                    tt("pool", vbf[:PT, :], VG[:PT, :], lnbb[:PT, :], ALU.add, ["VG", "lnbb"], ["vbf"])
                yield
                pm, pmn = nbank()
                for h in range(8):
                    hs = slice(h * 64, (h + 1) * 64)
                    mm(pm[:PT, hs], WsT[:PT, h, :PT], vbf[:PT, hs], True, False, ["WsT", "vbf"], [pmn])
                    mm(pm[:PT, hs], bsT[:, :PT], Eb[:, hs], False, True, ["bsT", "Eb"], [pmn])
                tt("dve", YA[:PT, :], U[:PT, :], pm[:PT, :], ALU.mult, ["U", pmn], ["YA"])
                act(yabf[:PT, :], YA[:PT, :], AF.Square, ["YA"], ["ssa", "yabf"], accum_out=stat[:PT, 4, 0:1])
                rsqrt_small(stat[:PT, 5, 0:1], stat[:PT, 4, 0:1], 1.0 / 512, ["ssa"], ["rsa"])
                ts("dve", yabf[:PT, :], YA[:PT, :], stat[:PT, 5, 0:1], None, ALU.mult, None, ["YA", "rsa"], ["yabf"])
                yield
                pstt, pn = npst()
                for kc in range(4):
                    tr(pstt[:, kc * PT:(kc + 1) * PT], yabf[:PT, kc * 128:(kc + 1) * 128], identb[:PT, :PT],
                       ["yabf", "identb"], [pn])
                cp("act", yT[:, 0:4, t_ * 128:t_ * 128 + PT], pstt[:, 0:4 * PT].rearrange("p (k t) -> p k t", k=4),
                   [pn], ["yTa"])
                yield

        def st_out(b):
            PT, NT = b.PT, b.NT
            for t_ in range(NT):
                tok = slice(t_ * 128, t_ * 128 + PT)
                for nh in range(2):
                    cols = slice(nh * 512, (nh + 1) * 512)
                    pa, pan = nbank()
                    for kc in range(4):
                        mm(pa[:PT, :], yT[:, kc, tok], Wout[:, kc, cols], kc == 0, kc == 3, [WOUT[kc], "yTa"], [pan])
                    pb_, pbn_ = nbank()
                    for kc in range(4, 8):
                        mm(pb_[:PT, :], yT[:, kc, tok], Wout[:, kc, cols], kc == 4, kc == 7, [WOUT[kc], "yTb%d" % (kc - 4)], [pbn_])
                    tt("dve", b.Xt[:PT, t_, cols], b.Xt[:PT, t_, cols], pa[:PT, :], ALU.add, [b.Xn[t_], pan], [b.Xn[t_]])
                    stt(b.Xt[:PT, t_, cols], pb_[:PT, :], stat[:PT, 8, t_:t_ + 1], b.Xt[:PT, t_, cols], ALU.mult, ALU.add,
                        [b.Xn[t_], pbn_, "rsb"], [b.Xn[t_]])
                yield

        def st_up(b):
            T = b.T
            for ug in range(16):
                rg, rgn = nring()
                rgv = rg[:].rearrange("p (k n) -> p k n", k=8)
                ld(rgv, scr_up[:, ug * 256:(ug + 1) * 256].rearrange("(k p) n -> p k n", p=128), [rgn], reads=["scr_up"], us=3.5)
                for f in range(2):
                    ffc = ug * 2 + f
                    ph, phn = mbank()
                    for kc in range(8):
                        mm(ph[:, :T], rgv[:, kc, f * 128:(f + 1) * 128], b.xT[:, kc, :T], kc == 0, kc == 7, [rgn, b.xTn], [phn])
                    R_, Rn = Rt[ffc % 2], "Rt%d" % (ffc % 2)
                    act(R_[:, :T], ph[:, :T], AF.Relu, [phn], [Rn])
                    tt("dve", arena[:, ffc, :T], ph[:, :T], R_[:, :T], ALU.mult, [phn, Rn], [AUn(ffc // 2)])
                    yield

        def st_down(b):
            PT, NT = b.PT, b.NT
            for nh in range(2):
                accs = [mbank() for _ in range(NT)]
                if NT < 4:
                    state["mb"] += 4 - NT
                for dg in range(8):
                    rg, rgn = nring()
                    rgv = rg[:].rearrange("p (f n) -> p f n", f=4)
                    ld(rgv, scr_dn[dg * 512:(dg + 1) * 512, nh * 512:(nh + 1) * 512].rearrange("(f p) n -> p f n", p=128),
                       [rgn], reads=["scr_dn"], us=3.5)
                    for t_ in range(NT):
                        tok = slice(t_ * 128, t_ * 128 + PT)
                        for f in range(4):
                            ffc = dg * 4 + f
                            mm(accs[t_][0][:PT, :], arena[:, ffc, tok], rgv[:, f, :], dg == 0 and f == 0, dg == 7 and f == 3,
                               [AUn(ffc // 2), rgn], [accs[t_][1]])
                        if t_ % 2 == 1:
                            yield
                    if NT == 1:
                        yield
                for t_ in range(NT):
                    tt("dve", b.Xt[:PT, t_, nh * 512:(nh + 1) * 512], b.Xt[:PT, t_, nh * 512:(nh + 1) * 512], accs[t_][0][:PT, :],
                       ALU.add, [b.Xn[t_], accs[t_][1]], [b.Xn[t_]])
                yield

        def st_final(b):
            PT, NT = b.PT, b.NT
            for t_ in range(NT):
                act(Rtt[:PT].rearrange("p a b -> p (a b)"), b.Xt[:PT, t_, :], AF.Square, [b.Xn[t_]], ["ssqf", "Rt0", "Rt1"], accum_out=stat[:PT, 6, t_:t_ + 1])
            rsqrt_small(stat[:PT, 7, 0:NT], stat[:PT, 6, 0:NT], 1.0 / D, ["ssqf"], ["rstdf"])
            for t_ in range(NT):
                stt(b.Xt[:PT, t_, :], b.Xt[:PT, t_, :], stat[:PT, 7, t_:t_ + 1], gfb[:PT, :], ALU.mult, ALU.mult,
                    [b.Xn[t_], "rstdf", "gfb"], [b.Xn[t_]])
            if b.sample:
                ld(ysm, b.Xt[:PT, 0, :], [], reads=b.Xn)
            else:
                jo = b.j - NPRE
                ld(y[jo * TB:(jo + 1) * TB, :].rearrange("(t p) d -> p t d", p=128), b.Xt[:], [], reads=b.Xn, us=9.0)
            yield

        def store_state(oconv, oh, T):
            for fc in range(4):
                ld(oconv[:, fc * 128:(fc + 1) * 128].rearrange("k p -> p k"), XB[:, fc, T:T + 3], [], reads=["XB%d" % fc], slow=True)
            ld(oh.rearrange("(c p) -> p c", p=128), Hst[:], [], reads=["Hst"], slow=True)

        def run(*gens):
            for g in gens:
                P.cur_tag = "%s" % getattr(g, "__name__", "?")
                for _ in g:
                    pass

        def chain(*gens):
            for g in gens:
                for _ in g:
                    yield

        def interleave(ga, gb, na=1, nb=1):
            da = db = False
            while not (da and db):
                for _ in range(na):
                    if not da:
                        try:
                            next(ga)
                        except StopIteration:
                            da = True
                for _ in range(nb):
                    if not db:
                        try:
                            next(gb)
                        except StopIteration:
                            db = True

        blks = [mkblk(j, "prefix" if j < NPRE else "full") for j in range(NBLK)]
        for j in range(NPRE):
            b = blks[j]
            st_load(b)
            run(st_norm(b, None, None), st_b1(b), st_b2(b))
            if deferred:
                o_, i_, n_ = deferred.pop(0)
                ld(o_, i_, [n_], eng="pool", us=12.0, issue=9.0, reads=["tok%d_0" % j])
        while deferred:
            o_, i_, n_ = deferred.pop(0)
            ld(o_, i_, [n_], eng="pool", us=12.0, issue=9.0)
        memset("pool", ex15[:, 0:1], 0.0, ["ring0", "ring1", "ring2", "yTa", "yTb0", "yTb1", "yTb2", "yTb3", "U", "VG", "YA", "Rt0", "Rt1"] + ["bs%d" % k for k in range(6)] + ["s1u%d" % k for k in range(16)],
               reads=["s1u%d" % k for k in range(16)])
        for kc in range(4):
            ts("dve", Wout[:, kc, :], Wout[:, kc, :], gnafm[:, kc:kc + 1], None, ALU.mult, None, ["Wout%d" % kc, "gnafm"], ["Wout%d" % kc])
        for j in range(NPRE, NBLK):
            b = blks[j]
            P.blk = j
            run(st_norm1s(b), st_a(b), st_bf(b))
            st_load(b)
            run(st_out(b), st_norm(b, g2b, "g2b"), st_up(b), st_down(b), st_final(b))
        store_state(oconv_p, oh_p, TB)
        bs = mkblk(0, "sample")
        st_load(bs)
        run(st_norm(bs, None, None), st_a(bs), st_bf(bs), st_out(bs), st_norm(bs, g2b, "g2b"),
            st_up(bs), st_down(bs), st_final(bs))
        store_state(oconv_s, oh_s, TS)
        sim = P.schedule()
        print("[sched] simulated time %.1f us, %d ops" % (sim, len(P.ops)))
        P.finalize()
    return nc


_NC_CACHE = {}


def _consts():
    ident = np.eye(128, dtype=np.float32)
    tril = np.tril(np.ones((128, 128), np.float32))
    E = np.zeros((64, 512), np.float32)
    for h in range(8):
        E[h, h * 64:(h + 1) * 64] = 1.0
        E[32 + h, h * 64:(h + 1) * 64] = 1.0
    return ident, tril, E


def kernel(x_prompt, x_sample, state_conv_b, state_h_b, norm1_g, w_in, ln_v_g, ln_v_b, w_s, b_s,
           conv_w, conv_b, w_a, b_a, w_x, b_x, lam, gn_a_g, gn_b_g, w_out, norm2_g, w_up, w_down, normf_g):
    f = lambda a: np.ascontiguousarray(np.asarray(a, dtype=np.float32))
    x_prompt = f(x_prompt)
    x_sample = f(x_sample)
    if "nc" not in _NC_CACHE:
        _NC_CACHE["nc"] = build_nc()
    nc = _NC_CACHE["nc"]
    ident, tril, E = _consts()
    shared = dict(
        c_ident=ident, c_tril=tril, c_E=E,
        norm1_g=f(norm1_g[0]), w_in=f(w_in[0]), ln_v_g=f(ln_v_g[0]), ln_v_b=f(ln_v_b[0]), w_s=f(w_s[0]), b_s=f(b_s[0]),
        conv_w=f(conv_w[0]), conv_b=f(conv_b[0]), w_a=f(w_a[0]), b_a=f(b_a[0]), w_x=f(w_x[0]), b_x=f(b_x[0]),
        lam=f(lam[0]), gn_a_g=f(gn_a_g[0]), gn_b_g=f(gn_b_g[0]), w_out=f(w_out[0]), norm2_g=f(norm2_g[0]),
        w_up=f(w_up[0]), w_down=f(w_down[0]), normf_g=f(normf_g),
    )
    in_maps = []
    for c in range(8):
        b, s = c // 4, c % 4
        npad = NPRE - 8 * s
        xs = np.zeros((NBLK * TB, D), np.float32)
        xs[npad * TB:] = x_prompt[b, :(s + 1) * NOWN * TB]
        keep = np.ones(32, np.float32)
        keep[:npad + 1] = 0.0
        first = np.zeros(32, np.float32)
        first[npad] = 1.0
        flags = np.concatenate([keep, first, 1.0 - first]).astype(np.float32)
        m = dict(shared)
        m.update(xs=xs, xsm=f(x_sample[c]), sconv=f(state_conv_b[0, c]), sh=f(state_h_b[0, c]), flags=flags)
        in_maps.append(m)
    res = run_bass_kernel_spmd(nc, in_maps, core_ids=list(range(8)))
    r = res.results
    y_prompt = np.stack([np.concatenate([r[b * 4 + s]["y"] for s in range(4)], axis=0) for b in range(2)])
    y_sample = np.stack([r[c]["ysm"] for c in range(8)])
    new_conv_p = np.stack([r[b * 4 + 3]["oconv_p"] for b in range(2)])[None]
    new_h_p = np.stack([r[b * 4 + 3]["oh_p"] for b in range(2)])[None]
    new_conv_s = np.stack([r[c]["oconv_s"] for c in range(8)])[None]
    new_h_s = np.stack([r[c]["oh_s"] for c in range(8)])[None]
    new_v_s = np.stack([r[c]["ov_s"] for c in range(8)])[None]
    return (y_prompt.astype(np.float32), y_sample.astype(np.float32), new_conv_p.astype(np.float32),
            new_h_p.astype(np.float32), new_conv_s.astype(np.float32), new_h_s.astype(np.float32),
            new_v_s.astype(np.float32))
```

```python
import numpy as np
import concourse.bass as bass
import concourse.mybir as mybir
from concourse.bass_utils import run_bass_kernel_spmd
from contextlib import ExitStack

F32 = mybir.dt.float32
BF16 = mybir.dt.bfloat16
AF = mybir.ActivationFunctionType
ALU = mybir.AluOpType

D = 1024
TB = 512
NPRE = 24
NOWN = 8
NBLK = NPRE + NOWN
TS = 32
EPS = 1e-6
SAME_DIST = 10 ** 9


class Prog:
    ENGS = ["pe", "act", "dve", "pool", "sp"]
    LAT = 1.0
    TABLE_US = 1.3

    def __init__(self, nc, es, rings):
        self.nc = nc
        self.ops = []
        self.last_w = {}
        self.readers = {}
        self.eng_ops = {e: [] for e in self.ENGS}
        self.R = rings
        self.csem = {e: es.enter_context(nc.semaphore("c_" + e)) for e in ["pe", "act", "dve", "pool"]}
        self.dsem = {e: [es.enter_context(nc.semaphore("d_%s%d" % (e, i))) for i in range(rings[e])]
                     for e in rings}

    def op(self, eng, fn, reads=(), writes=(), dma=False, dur=0.5, table=None, issue=0.1):
        i = len(self.ops)
        deps = set()
        for r in reads:
            if r in self.last_w:
                deps.add(self.last_w[r])
        for w in writes:
            if w in self.last_w:
                deps.add(self.last_w[w])
            deps.update(self.readers.get(w, ()))
        for r in reads:
            self.readers.setdefault(r, []).append(i)
        for w in writes:
            self.last_w[w] = i
            self.readers[w] = []
        deps.discard(i)
        self.ops.append(dict(eng=eng, fn=fn, deps=deps, dma=dma, signal=dma, dur=dur, table=table, issue=issue,
                             tag=(getattr(self, "blk", -1), getattr(self, "cur_tag", ""))))
        self.eng_ops[eng].append(i)
        return i

    def dma(self, eng, fn, reads=(), writes=(), dur=3.0, issue=0.1):
        return self.op(eng, fn, reads, writes, dma=True, dur=dur, issue=issue)

    def schedule(self):
        import heapq
        ops = self.ops
        n = len(ops)
        succ = [[] for _ in range(n)]
        indeg = [0] * n
        for i, o in enumerate(ops):
            indeg[i] = len(o["deps"])
            for d in o["deps"]:
                succ[d].append(i)
        bl = [0.0] * n
        for i in range(n - 1, -1, -1):
            m = 0.0
            for s_ in succ[i]:
                if bl[s_] > m:
                    m = bl[s_]
            bl[i] = m + ops[i]["dur"] + (self.LAT if succ[i] else 0.0)
        finish = [0.0] * n
        rt = [0.0] * n
        cand = {e: [] for e in self.ENGS}
        pend = {e: [] for e in self.ENGS}
        for i in range(n):
            if indeg[i] == 0:
                heapq.heappush(pend[ops[i]["eng"]], (0.0, i))
        free = {e: 0.0 for e in self.ENGS}
        order = {e: [] for e in self.ENGS}
        recent = {e: [] for e in self.ENGS}
        cur_table = [None]
        bus_free = 0.0
        left = n
        while left:
            best_e, best_t = None, None
            for e in self.ENGS:
                if cand[e]:
                    t = free[e]
                elif pend[e]:
                    t = max(free[e], pend[e][0][0])
                else:
                    continue
                if best_t is None or t < best_t:
                    best_e, best_t = e, t
            e, t = best_e, best_t
            while pend[e] and pend[e][0][0] <= t:
                r_, i_ = heapq.heappop(pend[e])
                heapq.heappush(cand[e], (-bl[i_], i_))
            pick = heapq.heappop(cand[e])
            if e != "pe" and cand[e]:
                rec = recent[e]

                def hazard(ix):
                    for d_ in ops[ix]["deps"]:
                        if d_ in rec:
                            return True
                    return False
                if hazard(pick[1]):
                    alt = []
                    found = None
                    while cand[e] and len(alt) < 16:
                        c = heapq.heappop(cand[e])
                        if not hazard(c[1]) and (-c[0]) > (-pick[0]) - 60.0:
                            found = c
                            break
                        alt.append(c)
                    for c in alt:
                        heapq.heappush(cand[e], c)
                    if found is not None:
                        heapq.heappush(cand[e], pick)
                        pick = found
            if e == "act" and len(cand[e]) > 0:
                def needs_switch(ix):
                    tb = ops[ix]["table"]
                    if tb is None:
                        return False
                    if tb == "tanh":
                        return cur_table[0] not in ("exp", "gelu")
                    return tb != cur_table[0]
                if needs_switch(pick[1]):
                    alt = []
                    found = None
                    while cand[e] and len(alt) < 24:
                        c = heapq.heappop(cand[e])
                        if not needs_switch(c[1]) and (-c[0]) > (-pick[0]) - 40.0:
                            found = c
                            break
                        alt.append(c)
                    for c in alt:
                        heapq.heappush(cand[e], c)
                    if found is not None:
                        heapq.heappush(cand[e], pick)
                        pick = found
            i = pick[1]
            o = ops[i]
            start = t
            if e != "pe":
                for d_ in o["deps"]:
                    if d_ in recent[e]:
                        start += 0.4
                        break
            if e == "act" and o["table"] is not None:
                tb = o["table"]
                if tb == "tanh":
                    if cur_table[0] not in ("exp", "gelu"):
                        start += self.TABLE_US
                        cur_table[0] = "exp"
                elif tb != cur_table[0]:
                    start += self.TABLE_US
                    cur_table[0] = tb
            if o["dma"]:
                free[e] = start + o["issue"]
                xs_ = max(start + o["issue"], bus_free)
                bus_free = xs_ + max(o["dur"] - 2.0, 0.1)
                finish[i] = xs_ + o["dur"]
            else:
                finish[i] = start + o["dur"]
                free[e] = finish[i]
            order[e].append(i)
            o["t0"] = start
            o["t1"] = finish[i]
            recent[e] = (recent[e] + [i])[-2:]
            left -= 1
            for s_ in succ[i]:
                lat = 0.0 if (ops[s_]["eng"] == e and not o["dma"]) else self.LAT
                v = finish[i] + lat
                if v > rt[s_]:
                    rt[s_] = v
                indeg[s_] -= 1
                if indeg[s_] == 0:
                    heapq.heappush(pend[ops[s_]["eng"]], (rt[s_], s_))
        self.eng_ops = order
        self.sim_time = max(finish) if n else 0.0
        return self.sim_time

    def finalize(self):
        ops = self.ops
        for e in self.ENGS:
            for k, i in enumerate(self.eng_ops[e]):
                ops[i]["lidx"] = k
        for o in ops:
            need = {}
            dd = []
            for d in o["deps"]:
                p = ops[d]
                if p["dma"]:
                    dd.append(d)
                    continue
                if p["eng"] == o["eng"]:
                    if o["eng"] == "pe":
                        continue
                    if o["lidx"] - p["lidx"] > SAME_DIST:
                        continue
                if p["eng"] not in need or ops[need[p["eng"]]]["lidx"] < p["lidx"]:
                    need[p["eng"]] = d
            o["cw"] = list(need.values())
            o["dw"] = dd
            for d in o["cw"]:
                ops[d]["signal"] = True
        for e in self.ENGS:
            c = 0
            n = 0
            for i in self.eng_ops[e]:
                o = ops[i]
                if o["dma"]:
                    o["slot"] = n % self.R[e]
                    o["val"] = 16 * (n // self.R[e] + 1)
                    o["sem"] = self.dsem[e][o["slot"]]
                    n += 1
                elif o["signal"]:
                    c += 1
                    o["val"] = c
                    o["sem"] = self.csem[e]

        def emit(e, eng):
            waited = {}

            def wait(sem, val):
                k = id(sem)
                if waited.get(k, 0) >= val:
                    return
                waited[k] = val
                eng.wait_ge(sem, val)

            last = {}
            for i in self.eng_ops[e]:
                o = ops[i]
                for d in o["cw"] + o["dw"]:
                    wait(ops[d]["sem"], ops[d]["val"])
                if o["dma"]:
                    if o["val"] > 16:
                        wait(o["sem"], o["val"] - 16)
                    ins = o["fn"](eng)
                    ins.then_inc(o["sem"], 16)
                    last[o["slot"]] = o
                else:
                    ins = o["fn"](eng)
                    if o["signal"]:
                        ins.then_inc(o["sem"], 1)
            for o in last.values():
                wait(o["sem"], o["val"])

        with self.nc.Block() as block:
            @block.tensor
            def _(eng):
                emit("pe", eng)

            @block.scalar
            def _(eng):
                emit("act", eng)

            @block.vector
            def _(eng):
                emit("dve", eng)

            @block.gpsimd
            def _(eng):
                emit("pool", eng)

            @block.sync
            def _(eng):
                emit("sp", eng)


def build_nc():
    nc = bass.Bass("TRN2", target_bir_lowering=False)

    def din(name, shape):
        return nc.dram_tensor(name, list(shape), F32, kind="ExternalInput").ap()

    def dout(name, shape):
        return nc.dram_tensor(name, list(shape), F32, kind="ExternalOutput").ap()

    xs = din("xs", [NBLK * TB, D])
    xsm = din("xsm", [TS, D])
    sconv = din("sconv", [3, 512])
    sh = din("sh", [512])
    flags = din("flags", [96])
    c_ident = din("c_ident", [128, 128])
    c_tril = din("c_tril", [128, 128])
    c_E = din("c_E", [64, 512])
    norm1_g = din("norm1_g", [D])
    w_in = din("w_in", [D, 2048])
    ln_v_g = din("ln_v_g", [512])
    ln_v_b = din("ln_v_b", [512])
    w_s = din("w_s", [8, 128, 128])
    b_s = din("b_s", [8, 128])
    conv_w = din("conv_w", [4, 512])
    conv_b = din("conv_b", [512])
    w_a = din("w_a", [8, 64, 64])
    b_a = din("b_a", [512])
    w_x = din("w_x", [8, 64, 64])
    b_x = din("b_x", [512])
    lam = din("lam", [512])
    gn_a_g = din("gn_a_g", [512])
    gn_b_g = din("gn_b_g", [512])
    w_out = din("w_out", [D, D])
    norm2_g = din("norm2_g", [D])
    w_up = din("w_up", [D, 4096])
    w_down = din("w_down", [4096, D])
    normf_g = din("normf_g", [D])

    y = dout("y", [NOWN * TB, D])
    ysm = dout("ysm", [TS, D])
    oconv_p = dout("oconv_p", [3, 512])
    oh_p = dout("oh_p", [512])
    oconv_s = dout("oconv_s", [3, 512])
    oh_s = dout("oh_s", [512])
    ov_s = dout("ov_s", [TS, 512])

    scr_up = nc.dram_tensor("scr_up", [D, 4096], BF16, kind="Internal").ap()
    scr_dn = nc.dram_tensor("scr_dn", [4096, D], BF16, kind="Internal").ap()

    with ExitStack() as es:
        def sb(name, shape, dt=F32):
            return es.enter_context(nc.sbuf_tensor(name, list(shape), dt))

        def pt(name, shape, dt=F32):
            return es.enter_context(nc.psum_tensor(name, list(shape), dt))

        P = Prog(nc, es, rings={"sp": 12, "pool": 40})

        Win = sb("Win", [128, 8, 2048], BF16)
        Wout = sb("Wout", [128, 8, 1024], BF16)
        WsT = sb("WsT", [128, 8, 128], BF16)
        identb = sb("identb", [128, 128], BF16)
        onesb = sb("onesb", [128, 128], BF16)
        Eb = sb("Eb", [64, 512], BF16)
        bsT = sb("bsT", [64, 128], BF16)
        g2b = sb("g2b", [128, D])
        gfb = sb("gfb", [128, D])
        lngb = sb("lngb", [128, 512])
        lnbb = sb("lnbb", [128, 512])
        flg = sb("flg", [128, 96])
        cwh = sb("cwh", [128, 4, 4])
        fmv = sb("fmv", [128, 8, 4])
        g1fm = sb("g1fm", [128, 8])
        gnafm = sb("gnafm", [128, 4])
        Hst = sb("Hst", [128, 4])
        h0t = sb("h0t", [128, 4])
        XB = sb("XB", [128, 4, TB + 3])
        X = [sb("X0", [128, 4, D]), sb("X1", [128, 4, D])]
        xnb0 = sb("xnb0", [128, D], BF16)
        xnb = [xnb0, xnb0]
        xnb1 = sb("xnb1", [128, D], BF16)
        xnT = [sb("xnT0", [128, 8, TB], BF16), sb("xnT1", [128, 8, TB], BF16)]
        yT = sb("yT", [128, 8, TB], BF16)
        U = sb("U", [128, 512])
        VG = sb("VG", [128, 512])
        YA = sb("YA", [128, 512])
        vbf = sb("vbf", [128, 512], BF16)
        yabf = sb("yabf", [128, 512], BF16)
        Rtt = sb("Rtt", [128, 2, 512], BF16)
        Rt = [Rtt[:, 0, :], Rtt[:, 1, :]]
        WA32 = sb("WA32", [128, 4, 128])
        WX32 = sb("WX32", [128, 4, 128])
        xcb = sb("xcb", [128, 4, 512], BF16)
        stat = sb("stat", [128, 16, 8])
        bnst = sb("bnst", [128, 6])
        arena = sb("arena", [128, 32, TB], BF16)
        ring = [sb("ring%d" % i, [128, 2048], BF16) for i in range(3)]
        bsc = sb("bsc", [128, 6, 512])
        stg = X[1][:, 0, :].rearrange("p (a b) -> p a b", a=8)
        STG = "X1t0"

        def AU(u):
            return arena[:, 2 * u:2 * u + 2, :].bitcast(F32).rearrange("p a b -> p (a b)")

        def AUn(u):
            return "ar%d" % u

        ex15 = Rtt[:].bitcast(F32).rearrange("p a b -> p (a b)")
        cst = sb("cst", [128, 2])

        arena_u = arena[:].bitcast(F32).rearrange("p (u a) b -> p u (a b)", a=2)

        def PHYS(k):
            return (k % 4) * 4 + (k // 4)

        class SC0:
            @staticmethod
            def u(k):
                return arena_u[:, PHYS(k), :]

            @staticmethod
            def n(k):
                return "ar%d" % PHYS(k)

            @staticmethod
            def group(k0, T):
                kind = k0 // 4
                v = arena[:].bitcast(F32).rearrange("p (f k a) b -> p f k (a b)", k=4, a=2)
                return v[:, :, kind, :T], ["ar%d" % PHYS(k0 + i) for i in range(4)]

            @staticmethod
            def sqrt_groups(T):
                return [SC0.group(4, T)]

            @staticmethod
            def half_bf(k):
                return arena[:, 2 * PHYS(k), :]

        class SC1:
            @staticmethod
            def u(k):
                if k < 6:
                    return ring[k // 2][:, 1024 * (k % 2):1024 * (k % 2 + 1)].bitcast(F32)
                if k < 8:
                    return bsc[:, k - 6, :]
                if k < 12:
                    kk = k - 8
                    return yT[:, 2 * kk:2 * kk + 2, :].bitcast(F32).rearrange("p a b -> p (a b)")
                return [U[:], VG[:], YA[:], ex15][k - 12]

            @staticmethod
            def n(k):
                return "s1u%d" % k

            @staticmethod
            def group(k0, T):
                return None

            @staticmethod
            def sqrt_groups(T):
                v = ring[2][:].bitcast(F32).rearrange("p (a b) -> p a b", a=2)
                return [(v[:, :, :T], ["s1u4", "s1u5"]), (bsc[:, 0:2, :T], ["s1u6", "s1u7"])]

        class SCF:
            @staticmethod
            def u(kind, fc):
                return bsc[:, 3 * (fc % 2) + kind, :]

            @staticmethod
            def n(kind, fc):
                return "bs%d" % (3 * (fc % 2) + kind)

            @staticmethod
            def pair(kind, T):
                v = bsc[:].rearrange("p (f k) n -> p f k n", k=3)
                return v[:, :, kind, :T], ["bs%d" % kind, "bs%d" % (3 + kind)]

        pstb = [pt("pst0", [128, 1024], BF16)]
        pbank = [pt("pb%d" % i, [128, 512], F32) for i in range(7)]
        state = {"pb": 0, "ring": 0, "mb": 0}
        MIX_BANKS = [0, 1, 2]
        MLP_BANKS = [3, 4, 5, 6]

        def nbank():
            k = MIX_BANKS[state["pb"] % 3]
            state["pb"] += 1
            return pbank[k], "pb%d" % k

        def mbank():
            k = MLP_BANKS[state["mb"] % 4]
            state["mb"] += 1
            return pbank[k], "pb%d" % k

        def npst():
            return pstb[0], "pst0"

        def nring():
            k = state["ring"]
            state["ring"] = (k + 1) % 3
            return ring[k], "ring%d" % k

        def fsz(ap):
            n = 1
            for d_ in ap.shape[1:]:
                n *= d_
            return n

        TABLE = {AF.Exp: "exp", AF.Tanh: "tanh", AF.Gelu_apprx_tanh: "gelu", AF.Sqrt: "sqrt", AF.Ln: "ln"}

        def act(out, in_, func, reads, writes, **kw):
            P.op("act", lambda e: e.activation(out=out, in_=in_, func=func, **kw), reads, writes,
                 dur=0.2 + fsz(in_) / 1000.0, table=TABLE.get(func))

        def edur(eng, n):
            if eng == "pool":
                return 0.2 + n / 600.0
            return 0.15 + n / 850.0

        def ts(eng, out, in0, s1, s2, op0, op1, reads, writes):
            if s2 is None:
                P.op(eng, lambda e: e.tensor_scalar(out=out, in0=in0, scalar1=s1, scalar2=None, op0=op0), reads, writes,
                     dur=edur(eng, fsz(out)))
            else:
                P.op(eng, lambda e: e.tensor_scalar(out=out, in0=in0, scalar1=s1, scalar2=s2, op0=op0, op1=op1), reads, writes,
                     dur=edur(eng, fsz(out)))

        def tt(eng, out, in0, in1, op, reads, writes):
            P.op(eng, lambda e: e.tensor_tensor(out=out, in0=in0, in1=in1, op=op), reads, writes,
                 dur=(0.2 + fsz(out) / 500.0) if eng == "pool" else edur(eng, fsz(out)))

        def stt(out, in0, scalar, in1, op0, op1, reads, writes):
            P.op("dve", lambda e: e.scalar_tensor_tensor(out=out, in0=in0, scalar=scalar, in1=in1, op0=op0, op1=op1), reads, writes,
                 dur=0.15 + fsz(out) / 850.0)

        def scan(out, d0, d1, init, reads, writes):
            P.op("dve", lambda e: e.tensor_tensor_scan(out=out, data0=d0, data1=d1, initial=init, op0=ALU.mult, op1=ALU.add),
                 reads, writes, dur=0.2 + fsz(out) / 480.0)

        def recip(out, in_, reads, writes):
            P.op("dve", lambda e: e.reciprocal(out=out, in_=in_), reads, writes, dur=0.15 + fsz(out) / 850.0)

        def cp(eng, out, in_, reads, writes):
            if eng == "act":
                act(out, in_, AF.Copy, reads, writes)
            else:
                n = fsz(out)
                d_ = (0.2 + n / 280.0) if eng == "pool" else (0.15 + n / 850.0)
                P.op(eng, lambda e: e.tensor_copy(out=out, in_=in_), reads, writes, dur=d_)

        def mm(out, lhsT, rhs, start, stop, reads, writes, f32=False):
            P.op("pe", lambda e: e.matmul(out, lhsT=lhsT, rhs=rhs, start=start, stop=stop), reads, writes,
                 dur=(0.02 + fsz(rhs) / 2300.0) * (4.0 if f32 else 1.0))

        def tr(out, in_, ident, reads, writes):
            P.op("pe", lambda e: e.transpose(out, in_, ident), reads, writes, dur=0.1)

        def ld(out, in_, writes, reads=(), eng="sp", slow=False, us=3.0, issue=0.1):
            if slow:
                P.dma(eng, lambda e: e.dma_start(out=out, in_=in_, allow_slow_non_contiguous=True), reads, writes, dur=us, issue=issue)
            else:
                P.dma(eng, lambda e: e.dma_start(out=out, in_=in_), reads, writes, dur=us, issue=issue)

        def memset(eng, ap, val, writes, reads=()):
            P.op(eng, lambda e: e.memset(ap, val), reads, writes, dur=0.2 + fsz(ap) / 1000.0)

        def ppow(out, in0, col, reads, writes):
            ex = cst[:out.shape[0], col:col + 1].to_broadcast(list(out.shape))
            P.op("pool", lambda e: e.tensor_tensor(out=out, in0=in0, in1=ex, op=ALU.pow), list(reads) + ["cst"], writes,
                 dur=0.3 + fsz(out) * 0.16)

        def rsqrt_small(dst, src, scale, reads, writes):
            ts("pool", dst, src, scale, EPS, ALU.mult, ALU.add, reads, writes)
            ppow(dst, dst, 0, writes, writes)

        memset("pool", cst[:, 0:1], -0.5, ["cst"])
        memset("pool", cst[:, 1:2], 0.5, ["cst"], reads=["cst"])
        memset("pool", onesb[:], 1.0, ["onesb"])
        memset("pool", bsT[:], 0.0, ["bsT"])
        memset("pool", Hst[:], 0.0, ["Hst"])
        memset("pool", XB[:], 0.0, ["XB0", "XB1", "XB2", "XB3"])
        ld(g1fm[:], norm1_g.rearrange("(c p) -> p c", p=128), ["g1fm"], slow=True)
        stgW = arena[:].bitcast(F32).rearrange("p (s a) b -> p s (a b)", s=4)
        for kc in range(8):
            sl = kc % 4
            nm_ = ["ar%d" % (sl * 4 + i) for i in range(4)]
            ld(stgW[:, sl, :], w_in[kc * 128:(kc + 1) * 128, :], nm_, us=4.5)
            if kc % 2 == 0:
                ts("dve", Win[:, kc, :], stgW[:, sl, :], g1fm[:, kc:kc + 1], None, ALU.mult, None, nm_ + ["g1fm"], ["Win%d" % kc])
            else:
                act(Win[:, kc, :], stgW[:, sl, :], AF.Copy, nm_ + ["g1fm"], ["Win%d" % kc], scale=g1fm[:, kc:kc + 1])
        deferred = []
        for kc in range(8):
            deferred.append((Wout[:, kc, :], w_out[kc * 128:(kc + 1) * 128, :], "Wout%d" % kc))
        for i in range(8):
            deferred.append((scr_up[i * 128:(i + 1) * 128, :], w_up[i * 128:(i + 1) * 128, :], "scr_up"))
        for i in range(8):
            deferred.append((scr_dn[i * 512:(i + 1) * 512, :], w_down[i * 512:(i + 1) * 512, :], "scr_dn"))

        ld(g2b[:], norm2_g.partition_broadcast(128), ["g2b"])
        ld(gfb[:], normf_g.partition_broadcast(128), ["gfb"])
        ld(lngb[:], ln_v_g.partition_broadcast(128), ["lngb"])
        ld(lnbb[:], ln_v_b.partition_broadcast(128), ["lnbb"])
        ld(flg[:], flags.partition_broadcast(128), ["flg"])
        ld(gnafm[:], gn_a_g.rearrange("(c p) -> p c", p=128), ["gnafm"], slow=True)
        for kc in range(8):
            pass
        WIN = ["Win%d" % kc for kc in range(8)]
        WOUT = ["Wout%d" % kc for kc in range(8)]

        ld(stg[:, 0, :], c_ident, [STG])
        cp("dve", identb[:], stg[:, 0, :], [STG], ["identb"])
        ld(YA[0:64, :], c_E, ["YA"])
        cp("dve", Eb[:], YA[0:64, :], ["YA"], ["Eb"])
        ld(VG[0:8, 0:128], b_s, ["VG"])
        ld(VG[32:40, 128:256], b_s, ["VG"])
        cp("dve", bsT[0:8, :], VG[0:8, 0:128], ["VG"], ["bsT"])
        cp("dve", yabf[32:40, 0:128], VG[32:40, 128:256], ["VG"], ["yabf"])
        cp("dve", VG[32:40, 256:384], yabf[32:40, 0:128], ["yabf"], ["VG"])
        tt("dve", VG[32:40, 128:256], VG[32:40, 128:256], VG[32:40, 256:384], ALU.subtract, ["VG"], ["VG"])
        cp("dve", bsT[32:40, :], VG[32:40, 128:256], ["VG"], ["bsT"])

        ld(stg[:], w_s.rearrange("h t s -> t h s"), [STG], reads=[STG])
        ld(U[:, 0:128], c_tril, ["U"])
        for h in range(8):
            tt("dve", YA[:, 0:128], stg[:, h, :], U[:, 0:128], ALU.mult, [STG, "U"], ["YA"])
            cp("dve", vbf[:, 0:128], YA[:, 0:128], ["YA"], ["vbf"])
            pstt, pn = npst()
            tr(pstt[:, 0:128], vbf[:, 0:128], identb[:], ["vbf", "identb"], [pn])
            cp("dve", WsT[:, h, :], pstt[:, 0:128], [pn], ["WsT"])

        for (wsrc, w32, nm) in ((w_a, WA32, "WA"), (w_x, WX32, "WX")):
            memset("pool", w32[:], 0.0, [nm + "32"])
            for h in range(8):
                po = (h % 2) * 64
                ld(w32[po:po + 64, h // 2, po:po + 64], wsrc[h], [nm + "32"], reads=[nm + "32"])

        def fml(dst, src, nm):
            ld(dst, src.rearrange("(c p) -> p c", p=128), [nm], slow=True)
        for k in range(4):
            fml(cwh[:, k, :], conv_w[k], "cwh")
        fml(fmv[:, 0, :], conv_b, "fmv0")
        fml(fmv[:, 1, :], b_a, "fmv1")
        fml(fmv[:, 2, :], b_x, "fmv2")
        fml(fmv[:, 3, :], lam, "fmv3")
        fml(fmv[:, 4, :], gn_b_g, "fmv4")
        ts("dve", cwh[:].rearrange("p a b -> p (a b)"), cwh[:].rearrange("p a b -> p (a b)"), 0.5, None, ALU.mult, None, ["cwh"], ["cwh"])
        for i in range(3):
            ts("dve", fmv[:, i, :], fmv[:, i, :], 0.5, None, ALU.mult, None, ["fmv%d" % i], ["fmv%d" % i])
        act(fmv[:, 7, :], fmv[:, 3, :], AF.Exp, ["fmv3"], ["fmv7"], scale=-1.0)
        act(fmv[:, 7, :], fmv[:, 7, :], AF.Ln, ["fmv7"], ["fmv7"], bias=1.0)
        ts("dve", fmv[:, 5, :], fmv[:, 7, :], -4.0, None, ALU.mult, None, ["fmv7"], ["fmv5"])
        ts("dve", fmv[:, 6, :], fmv[:, 7, :], -8.0, None, ALU.mult, None, ["fmv7"], ["fmv6"])

        class Blk:
            pass

        def mkblk(j, kind):
            b = Blk()
            b.j = j
            b.sample = kind == "sample"
            b.full = kind != "prefix"
            b.T = TS if b.sample else TB
            b.PT = TS if b.sample else 128
            b.NT = 1 if b.sample else 4
            b.slot = 0 if b.sample else j % 2
            b.Xt = X[b.slot]
            b.Xn = ["X%dt%d" % (b.slot, t_) for t_ in range(b.NT)]
            b.xT = xnT[b.slot]
            b.xTn = "xnT%d" % b.slot
            b.sc = SC1 if (kind == "prefix" and j % 2 == 1) else SC0
            return b

        def st_load(b):
            if b.sample:
                ld(b.Xt[:b.PT, 0, :], xsm, b.Xn)
            else:
                ld(b.Xt[:], xs[b.j * TB:(b.j + 1) * TB, :].rearrange("(t p) d -> p t d", p=128), b.Xn, us=9.0)

        def st_norm(b, gb, gname):
            PT, NT = b.PT, b.NT
            for t_ in range(NT):
                act(xnb0[:PT, :], b.Xt[:PT, t_, :], AF.Square, [b.Xn[t_]], ["ssq", "xnb0"], accum_out=stat[:PT, 0, t_:t_ + 1])
            rsqrt_small(stat[:PT, 1, 0:NT], stat[:PT, 0, 0:NT], 1.0 / D, ["ssq"], ["rstd"])
            yield

            def scale(t_):
                xb_, xbn = xnb[t_ % 2], "xnb0"
                if gb is None:
                    ts("dve", xb_[:PT, :], b.Xt[:PT, t_, :], stat[:PT, 1, t_:t_ + 1], None, ALU.mult, None,
                       [b.Xn[t_], "rstd"], [xbn])
                else:
                    stt(xb_[:PT, :], b.Xt[:PT, t_, :], stat[:PT, 1, t_:t_ + 1], gb[:PT, :], ALU.mult, ALU.mult,
                        [b.Xn[t_], "rstd", gname], [xbn])

            def trans(t_):
                xb_, xbn = xnb[t_ % 2], "xnb0"
                if gb is not None:
                    mb_, pn = mbank()
                    pstt = mb_[:].bitcast(BF16)
                else:
                    pstt, pn = npst()
                for kc in range(8):
                    tr(pstt[:, kc * PT:(kc + 1) * PT], xb_[:PT, kc * 128:(kc + 1) * 128], identb[:PT, :PT],
                       [xbn, "identb"], [pn])
                eng = "act" if t_ % 2 == 0 else "dve"
                cp(eng, b.xT[:, :, t_ * 128:t_ * 128 + PT], pstt[:, 0:8 * PT].rearrange("p (k t) -> p k t", k=8),
                   [pn], [b.xTn])

            scale(0)
            yield
            for t_ in range(1, NT):
                trans(t_ - 1)
                scale(t_)
                yield
            trans(NT - 1)
            yield

        xstg = xcb[:].bitcast(F32).rearrange("p a b -> p (a b)")
        XSTG = ["xcb0", "xcb1", "xcb2", "xcb3"]

        def st_norm1s(b):
            for t_ in range(4):
                sq, rs = "ssq1_%d" % t_, "rstd1_%d" % t_
                ld(xstg, xs[b.j * TB + t_ * 128:b.j * TB + (t_ + 1) * 128, :], XSTG, us=3.5)
                act(xnb1[:, :], xstg, AF.Square, XSTG, [sq, "xnb1"], accum_out=stat[:, 9, t_:t_ + 1])
                ts("pool", stat[:, 10, t_:t_ + 1], stat[:, 9, t_:t_ + 1], 1.0 / D, EPS, ALU.mult, ALU.add, [sq], [rs])
                ppow(stat[:, 10, t_:t_ + 1], stat[:, 10, t_:t_ + 1], 0, [rs], [rs])
                ts("dve", xnb1[:, :], xstg, stat[:, 10, t_:t_ + 1], None, ALU.mult, None, XSTG + [rs], ["xnb1"])
                pstt, pn = npst()
                for kc in range(8):
                    tr(pstt[:, kc * 128:(kc + 1) * 128], xnb1[:, kc * 128:(kc + 1) * 128], identb[:, :],
                       ["xnb1", "identb"], [pn])
                eng = "act" if t_ % 2 == 0 else "dve"
                cp(eng, b.xT[:, :, t_ * 128:(t_ + 1) * 128], pstt[:, 0:1024].rearrange("p (k t) -> p k t", k=8),
                   [pn], [b.xTn])
                yield

        def st_b1(b):
            T, j = b.T, b.j
            AU, AUn = b.sc.u, b.sc.n
            merged = b.sc.group(4, T) is not None
            if b.sample:
                for fc in range(4):
                    ld(XB[:, fc, 0:3], sconv[:, fc * 128:(fc + 1) * 128].rearrange("k p -> p k"), ["XB%d" % fc], slow=True)
            pbs = []
            for fc in range(4):
                pb, pbn = nbank()
                for kc in range(8):
                    mm(pb[:, :T], Win[:, kc, 1024 + fc * 128:1024 + (fc + 1) * 128], b.xT[:, kc, :T], kc == 0, kc == 7,
                       [WIN[kc], b.xTn], [pbn])
                if not b.sample and j > 0:
                    cp("act", XB[:, fc, 0:3], XB[:, fc, TB:TB + 3], ["XB%d" % fc], ["XB%d" % fc])
                cp("act", XB[:, fc, 3:3 + T], pb[:, :T], [pbn], ["XB%d" % fc])
                yield
            for fc in range(4):
                TMP, TMPn = AU(12 + fc), AUn(12 + fc)
                ts("pool", TMP[:, :T], XB[:, fc, 0:T], cwh[:, 0, fc:fc + 1], fmv[:, 0, fc:fc + 1], ALU.mult, ALU.add,
                   ["XB%d" % fc, "cwh", "fmv0"], [TMPn])
            yield
            for k in range(1, 4):
                for fc in range(4):
                    TMP, TMPn = AU(12 + fc), AUn(12 + fc)
                    stt(TMP[:, :T], XB[:, fc, k:k + T], cwh[:, k, fc:fc + 1], TMP[:, :T], ALU.mult, ALU.add,
                        ["XB%d" % fc, "cwh", TMPn], [TMPn])
                yield
            f32g = not b.full
            if not f32g:
                for fc in range(4):
                    cp("pool", xcb[:, fc, :T], AU(12 + fc)[:, :T], [AUn(12 + fc)], ["xcb%d" % fc])
            yield
            for fc in range(4):
                A_, T1_, V_ = AU(fc), AU(4 + fc), AU(8 + fc)
                An, T1n, Vn = AUn(fc), AUn(4 + fc), AUn(8 + fc)
                pr, prn = nbank()
                if f32g:
                    mm(pr[:, :T], WA32[:, fc, :], AU(12 + fc)[:, :T], True, True, ["WA32", AUn(12 + fc)], [prn], f32=True)
                else:
                    mm(pr[:, :T], WA[:, fc, :], xcb[:, fc, :T], True, True, ["WA", "xcb%d" % fc], [prn])
                act(T1_[:, :T], pr[:, :T], AF.Tanh, [prn, "fmv1"], [T1n], bias=fmv[:, 1, fc:fc + 1])
                pi, pin = nbank()
                if f32g:
                    mm(pi[:, :T], WX32[:, fc, :], AU(12 + fc)[:, :T], True, True, ["WX32", AUn(12 + fc)], [pin], f32=True)
                else:
                    mm(pi[:, :T], WX[:, fc, :], xcb[:, fc, :T], True, True, ["WX", "xcb%d" % fc], [pin])
                act(V_[:, :T], pi[:, :T], AF.Tanh, [pin, "fmv2"], [Vn], bias=fmv[:, 2, fc:fc + 1])
                yield
            for fc in range(4):
                A_, T1_, V_ = AU(fc), AU(4 + fc), AU(8 + fc)
                An, T1n, Vn = AUn(fc), AUn(4 + fc), AUn(8 + fc)
                TMP, TMPn = AU(12 + fc), AUn(12 + fc)
                act(A_[:, :T], T1_[:, :T], AF.Exp, [T1n, "fmv5"], [An], scale=fmv[:, 5, fc:fc + 1], bias=fmv[:, 5, fc:fc + 1])
                act(T1_[:, :T], T1_[:, :T], AF.Exp, [T1n, "fmv6"], [T1n], scale=fmv[:, 6, fc:fc + 1], bias=fmv[:, 6, fc:fc + 1])
                if merged:
                    continue
                ts("pool", T1_[:, :T], T1_[:, :T], -1.0, 1.0, ALU.mult, ALU.add, [T1n], [T1n])
                stt(V_[:, :T], V_[:, :T], 1.0, TMP[:, :T], ALU.add, ALU.mult, [Vn, TMPn], [Vn])
                if (not b.sample) and j % 8 == 0:
                    ts("pool", T1_[:, 0:1], T1_[:, 0:1], flg[:, 64 + j:64 + j + 1], flg[:, 32 + j:32 + j + 1],
                       ALU.mult, ALU.add, [T1n, "flg"], [T1n])
                yield
            if merged:
                gT1, nT1 = b.sc.group(4, T)
                gV, nV = b.sc.group(8, T)
                gTM, nTM = b.sc.group(12, T)
                ts("pool", gT1, gT1, -1.0, 1.0, ALU.mult, ALU.add, nT1, nT1)
                stt(gV, gV, 1.0, gTM, ALU.add, ALU.mult, nV + nTM, nV)
                if (not b.sample) and j % 8 == 0:
                    for fc in range(4):
                        T1_, T1n = AU(4 + fc), AUn(4 + fc)
                        ts("pool", T1_[:, 0:1], T1_[:, 0:1], flg[:, 64 + j:64 + j + 1], flg[:, 32 + j:32 + j + 1],
                           ALU.mult, ALU.add, [T1n, "flg"], [T1n])
                yield

        def st_b2(b):
            T, j = b.T, b.j
            AU, AUn = b.sc.u, b.sc.n
            if b.sample:
                ld(h0t[:], sh.rearrange("(c p) -> p c", p=128), ["h0t"], slow=True)
            else:
                ts("dve", h0t[:], Hst[:], flg[:, j:j + 1], None, ALU.mult, None, ["Hst", "flg"], ["h0t"])
            for (v_, n_) in b.sc.sqrt_groups(T):
                act(v_, v_, AF.Sqrt, n_, n_)
            yield
            if b.sc.group(4, T) is not None:
                gT1, nT1 = b.sc.group(4, T)
                gV, nV = b.sc.group(8, T)
                tt("dve", gV, gV, gT1, ALU.mult, nV + nT1, nV)
            else:
                for fc in range(4):
                    A_, T1_, V_ = AU(fc), AU(4 + fc), AU(8 + fc)
                    An, T1n, Vn = AUn(fc), AUn(4 + fc), AUn(8 + fc)
                    tt("dve", V_[:, :T], V_[:, :T], T1_[:, :T], ALU.mult, [Vn, T1n], [Vn])
            yield
            for fc in range(4):
                A_, T1_, V_ = AU(fc), AU(4 + fc), AU(8 + fc)
                An, T1n, Vn = AUn(fc), AUn(4 + fc), AUn(8 + fc)
                scan(T1_[:, :T], A_[:, :T], V_[:, :T], h0t[:, fc:fc + 1], [An, Vn, "h0t"], [T1n])
                cp("act", Hst[:, fc:fc + 1], T1_[:, T - 1:T], [T1n], ["Hst", "tok%d_%d" % (b.j, int(b.sample))])
                yield
            if not b.full:
                return
            for fc in range(4):
                A_, T1_, V_ = AU(fc), AU(4 + fc), AU(8 + fc)
                An, T1n, Vn = AUn(fc), AUn(4 + fc), AUn(8 + fc)
                TMP, TMPn = AU(12 + fc), AUn(12 + fc)
                pg, pgn = nbank()
                for kc in range(8):
                    mm(pg[:, :T], Win[:, kc, 1536 + fc * 128:1536 + (fc + 1) * 128], b.xT[:, kc, :T], kc == 0, kc == 7,
                       [WIN[kc], b.xTn], [pgn])
                act(TMP[:, :T], pg[:, :T], AF.Gelu_apprx_tanh, [pgn], [TMPn])
                stt(A_[:, :T], T1_[:, :T], fmv[:, 4, fc:fc + 1], TMP[:, :T], ALU.mult, ALU.mult, [T1n, "fmv4", TMPn], [An])
                ysq = SC0.half_bf(8 + fc)
                act(ysq[:, :T], A_[:, :T], AF.Square, [An], [Vn])
            yield
            pb, pbn = nbank()
            for fc in range(4):
                mm(pb[:, :T], onesb[:], SC0.half_bf(8 + fc)[:, :T], fc == 0, fc == 3, ["onesb", AUn(8 + fc)], [pbn])
            RB, RBn = AU(12), AUn(12)
            ts("dve", RB[:, :T], pb[:, :T], 1.0 / 512, EPS, ALU.mult, ALU.add, [pbn], [RBn])
            act(RB[:, :T], RB[:, :T], AF.Sqrt, [RBn], [RBn])
            recip(RB[:, :T], RB[:, :T], [RBn], [RBn])
            for fc in range(4):
                tt("pool", yT[:, 4 + fc, :T], AU(fc)[:, :T], RB[:, :T], ALU.mult, [AUn(fc), RBn], ["yTb"])
            yield

        def st_bf(b):
            T, j, PT, NT = b.T, b.j, b.PT, b.NT
            if b.sample:
                ld(h0t[:], sh.rearrange("(c p) -> p c", p=128), ["h0t"], slow=True)
                for fc in range(4):
                    ld(XB[:, fc, 0:3], sconv[:, fc * 128:(fc + 1) * 128].rearrange("k p -> p k"), ["XB%d" % fc], slow=True)
            else:
                ts("dve", h0t[:], Hst[:], flg[:, j:j + 1], None, ALU.mult, None, ["Hst", "flg"], ["h0t"])
            for half in range(2):
                fcs = (2 * half, 2 * half + 1)
                for fc in fcs:
                    pb, pbn = nbank()
                    for kc in range(8):
                        mm(pb[:, :T], Win[:, kc, 1024 + fc * 128:1024 + (fc + 1) * 128], b.xT[:, kc, :T], kc == 0, kc == 7,
                           [WIN[kc], b.xTn], [pbn])
                    if not b.sample and j > 0:
                        cp("act", XB[:, fc, 0:3], XB[:, fc, TB:TB + 3], ["XB%d" % fc], ["XB%d" % fc])
                    cp("act", XB[:, fc, 3:3 + T], pb[:, :T], [pbn], ["XB%d" % fc])
                yield
                for fc in fcs:
                    ts("pool", SCF.u(0, fc)[:, :T], XB[:, fc, 0:T], cwh[:, 0, fc:fc + 1], fmv[:, 0, fc:fc + 1], ALU.mult, ALU.add,
                       ["XB%d" % fc, "cwh", "fmv0"], [SCF.n(0, fc)])
                for k in range(1, 4):
                    for fc in fcs:
                        stt(SCF.u(0, fc)[:, :T], XB[:, fc, k:k + T], cwh[:, k, fc:fc + 1], SCF.u(0, fc)[:, :T], ALU.mult, ALU.add,
                            ["XB%d" % fc, "cwh", SCF.n(0, fc)], [SCF.n(0, fc)])
                yield
                for fc in fcs:
                    TM, TMn = SCF.u(0, fc), SCF.n(0, fc)
                    T1_, T1n = SCF.u(1, fc), SCF.n(1, fc)
                    V_, Vn = SCF.u(2, fc), SCF.n(2, fc)
                    pr, prn = nbank()
                    mm(pr[:, :T], WA32[:, fc, :], TM[:, :T], True, True, ["WA32", TMn], [prn], f32=True)
                    act(T1_[:, :T], pr[:, :T], AF.Tanh, [prn, "fmv1"], [T1n], bias=fmv[:, 1, fc:fc + 1])
                    pi, pin = nbank()
                    mm(pi[:, :T], WX32[:, fc, :], TM[:, :T], True, True, ["WX32", TMn], [pin], f32=True)
                    act(V_[:, :T], pi[:, :T], AF.Tanh, [pin, "fmv2"], [Vn], bias=fmv[:, 2, fc:fc + 1])
                    stt(V_[:, :T], V_[:, :T], 1.0, TM[:, :T], ALU.add, ALU.mult, [Vn, TMn], [Vn])
                    yield
                for fc in fcs:
                    A_, An = SCF.u(0, fc), SCF.n(0, fc)
                    T1_, T1n = SCF.u(1, fc), SCF.n(1, fc)
                    act(A_[:, :T], T1_[:, :T], AF.Exp, [T1n, "fmv5"], [An], scale=fmv[:, 5, fc:fc + 1], bias=fmv[:, 5, fc:fc + 1])
                    act(T1_[:, :T], T1_[:, :T], AF.Exp, [T1n, "fmv6"], [T1n], scale=fmv[:, 6, fc:fc + 1], bias=fmv[:, 6, fc:fc + 1])
                gT1, nT1 = SCF.pair(1, T)
                gV, nV = SCF.pair(2, T)
                ts("pool", gT1, gT1, -1.0, 1.0, ALU.mult, ALU.add, nT1, nT1)
                if (not b.sample) and j % 8 == 0:
                    for fc in fcs:
                        T1_, T1n = SCF.u(1, fc), SCF.n(1, fc)
                        ts("pool", T1_[:, 0:1], T1_[:, 0:1], flg[:, 64 + j:64 + j + 1], flg[:, 32 + j:32 + j + 1],
                           ALU.mult, ALU.add, [T1n, "flg"], [T1n])
                act(gT1, gT1, AF.Sqrt, nT1, nT1)
                tt("dve", gV, gV, gT1, ALU.mult, nV + nT1, nV)
                yield
                for fc in fcs:
                    A_, An = SCF.u(0, fc), SCF.n(0, fc)
                    T1_, T1n = SCF.u(1, fc), SCF.n(1, fc)
                    V_, Vn = SCF.u(2, fc), SCF.n(2, fc)
                    scan(T1_[:, :T], A_[:, :T], V_[:, :T], h0t[:, fc:fc + 1], [An, Vn, "h0t"], [T1n])
                    cp("act", Hst[:, fc:fc + 1], T1_[:, T - 1:T], [T1n], ["Hst", "tok%d_%d" % (b.j, int(b.sample))])
                    pg, pgn = nbank()
                    for kc in range(8):
                        mm(pg[:, :T], Win[:, kc, 1536 + fc * 128:1536 + (fc + 1) * 128], b.xT[:, kc, :T], kc == 0, kc == 7,
                           [WIN[kc], b.xTn], [pgn])
                    act(A_[:, :T], pg[:, :T], AF.Gelu_apprx_tanh, [pgn, An], [An])
                    stt(yT[:, 4 + fc, :T], T1_[:, :T], fmv[:, 4, fc:fc + 1], A_[:, :T], ALU.mult, ALU.mult, [T1n, "fmv4", An], ["yTb%d" % fc])
                    act(xcb[:, fc, :T], yT[:, 4 + fc, :T], AF.Square, ["yTb%d" % fc], ["xcb%d" % fc])
                    yield
            pk, pkn = nbank()
            for t_ in range(NT):
                tok = slice(t_ * 128, t_ * 128 + PT)
                for fc in range(4):
                    mm(pk[:PT, t_:t_ + 1], xcb[:, fc, tok], onesb[:, 0:1], fc == 0, fc == 3, ["xcb%d" % fc, "onesb"], [pkn])
            ts("dve", stat[:PT, 8, 0:NT], pk[:PT, 0:NT], 1.0 / 512, EPS, ALU.mult, ALU.add, [pkn], ["rsb"])
            ppow(stat[:PT, 8, 0:NT], stat[:PT, 8, 0:NT], 0, ["rsb"], ["rsb"])
            yield

        def st_a(b):
            PT, NT = b.PT, b.NT
            for t_ in range(NT):
                tok = slice(t_ * 128, t_ * 128 + PT)
                pu, pun = nbank()
                for kc in range(8):
                    mm(pu[:PT, :], b.xT[:, kc, tok], Win[:, kc, 0:512], kc == 0, kc == 7, [WIN[kc], b.xTn], [pun])
                pv, pvn = nbank()
                for kc in range(8):
                    mm(pv[:PT, :], b.xT[:, kc, tok], Win[:, kc, 512:1024], kc == 0, kc == 7, [WIN[kc], b.xTn], [pvn])
                act(U[:PT, :], pu[:PT, :], AF.Gelu_apprx_tanh, [pun], ["U"])
                act(VG[:PT, :], pv[:PT, :], AF.Gelu_apprx_tanh, [pvn], ["VG"])
                yield
                P.op("dve", lambda e: e.bn_stats(out=bnst[:PT, :], in_=VG[:PT, :]), ["VG"], ["bnst"], dur=0.75)
                P.op("dve", lambda e: e.bn_aggr(out=stat[:PT, 2, 0:2], in_=bnst[:PT, :]), ["bnst"], ["mv"], dur=0.2)
                rsqrt_small(stat[:PT, 3, 0:1], stat[:PT, 2, 1:2], 1.0, ["mv"], ["lnr"])
                stt(stat[:PT, 3, 1:2], stat[:PT, 2, 0:1], -1.0, stat[:PT, 3, 0:1], ALU.mult, ALU.mult, ["mv", "lnr"], ["lnb2"])
                act(VG[:PT, :], VG[:PT, :], AF.Identity, ["VG", "lnr", "lnb2"], ["VG"], scale=stat[:PT, 3, 0:1], bias=stat[:PT, 3, 1:2])
                tt("dve", VG[:PT, :], VG[:PT, :], lngb[:PT, :], ALU.mult, ["VG", "lngb"], ["VG"])
                if b.sample:
                    tt("pool", VG[:PT, :], VG[:PT, :], lnbb[:PT, :], ALU.add, ["VG", "lnbb"], ["VG"])
                    ld(ov_s, VG[:PT, :], [], reads=["VG"])
                    cp("pool", vbf[:PT, :], VG[:PT, :], ["VG"], ["vbf"])
                else:
                    tt("pool", vbf[:PT, :], VG[:PT, :], lnbb[:PT, :], ALU.add, ["VG", "lnbb"], ["vbf"])
                yield
                pm, pmn = nbank()
                for h in range(8):
                    hs = slice(h * 64, (h + 1) * 64)
                    mm(pm[:PT, hs], WsT[:PT, h, :PT], vbf[:PT, hs], True, False, ["WsT", "vbf"], [pmn])
                    mm(pm[:PT, hs], bsT[:, :PT], Eb[:, hs], False, True, ["bsT", "Eb"], [pmn])
                tt("dve", YA[:PT, :], U[:PT, :], pm[:PT, :], ALU.mult, ["U", pmn], ["YA"])
                act(yabf[:PT, :], YA[:PT, :], AF.Square, ["YA"], ["ssa", "yabf"], accum_out=stat[:PT, 4, 0:1])
                rsqrt_small(stat[:PT, 5, 0:1], stat[:PT, 4, 0:1], 1.0 / 512, ["ssa"], ["rsa"])
                ts("dve", yabf[:PT, :], YA[:PT, :], stat[:PT, 5, 0:1], None, ALU.mult, None, ["YA", "rsa"], ["yabf"])
                yield
                pstt, pn = npst()
                for kc in range(4):
                    tr(pstt[:, kc * PT:(kc + 1) * PT], yabf[:PT, kc * 128:(kc + 1) * 128], identb[:PT, :PT],
                       ["yabf", "identb"], [pn])
                cp("act", yT[:, 0:4, t_ * 128:t_ * 128 + PT], pstt[:, 0:4 * PT].rearrange("p (k t) -> p k t", k=4),
                   [pn], ["yTa"])
                yield

        def st_out(b):
            PT, NT = b.PT, b.NT
            for t_ in range(NT):
                tok = slice(t_ * 128, t_ * 128 + PT)
                for nh in range(2):
                    cols = slice(nh * 512, (nh + 1) * 512)
                    pa, pan = nbank()
                    for kc in range(4):
                        mm(pa[:PT, :], yT[:, kc, tok], Wout[:, kc, cols], kc == 0, kc == 3, [WOUT[kc], "yTa"], [pan])
                    pb_, pbn_ = nbank()
                    for kc in range(4, 8):
                        mm(pb_[:PT, :], yT[:, kc, tok], Wout[:, kc, cols], kc == 4, kc == 7, [WOUT[kc], "yTb%d" % (kc - 4)], [pbn_])
                    tt("dve", b.Xt[:PT, t_, cols], b.Xt[:PT, t_, cols], pa[:PT, :], ALU.add, [b.Xn[t_], pan], [b.Xn[t_]])
                    stt(b.Xt[:PT, t_, cols], pb_[:PT, :], stat[:PT, 8, t_:t_ + 1], b.Xt[:PT, t_, cols], ALU.mult, ALU.add,
                        [b.Xn[t_], pbn_, "rsb"], [b.Xn[t_]])
                yield

        def st_up(b):
            T = b.T
            for ug in range(16):
                rg, rgn = nring()
                rgv = rg[:].rearrange("p (k n) -> p k n", k=8)
                ld(rgv, scr_up[:, ug * 256:(ug + 1) * 256].rearrange("(k p) n -> p k n", p=128), [rgn], reads=["scr_up"], us=3.5)
                for f in range(2):
                    ffc = ug * 2 + f
                    ph, phn = mbank()
                    for kc in range(8):
                        mm(ph[:, :T], rgv[:, kc, f * 128:(f + 1) * 128], b.xT[:, kc, :T], kc == 0, kc == 7, [rgn, b.xTn], [phn])
                    R_, Rn = Rt[ffc % 2], "Rt%d" % (ffc % 2)
                    act(R_[:, :T], ph[:, :T], AF.Relu, [phn], [Rn])
                    tt("dve", arena[:, ffc, :T], ph[:, :T], R_[:, :T], ALU.mult, [phn, Rn], [AUn(ffc // 2)])
                    yield

        def st_down(b):
            PT, NT = b.PT, b.NT
            for nh in range(2):
                accs = [mbank() for _ in range(NT)]
                if NT < 4:
                    state["mb"] += 4 - NT
                for dg in range(8):
                    rg, rgn = nring()
                    rgv = rg[:].rearrange("p (f n) -> p f n", f=4)
                    ld(rgv, scr_dn[dg * 512:(dg + 1) * 512, nh * 512:(nh + 1) * 512].rearrange("(f p) n -> p f n", p=128),
                       [rgn], reads=["scr_dn"], us=3.5)
                    for t_ in range(NT):
                        tok = slice(t_ * 128, t_ * 128 + PT)
                        for f in range(4):
                            ffc = dg * 4 + f
                            mm(accs[t_][0][:PT, :], arena[:, ffc, tok], rgv[:, f, :], dg == 0 and f == 0, dg == 7 and f == 3,
                               [AUn(ffc // 2), rgn], [accs[t_][1]])
                        if t_ % 2 == 1:
                            yield
                    if NT == 1:
                        yield
                for t_ in range(NT):
                    tt("dve", b.Xt[:PT, t_, nh * 512:(nh + 1) * 512], b.Xt[:PT, t_, nh * 512:(nh + 1) * 512], accs[t_][0][:PT, :],
                       ALU.add, [b.Xn[t_], accs[t_][1]], [b.Xn[t_]])
                yield

        def st_final(b):
            PT, NT = b.PT, b.NT
            for t_ in range(NT):
                act(Rtt[:PT].rearrange("p a b -> p (a b)"), b.Xt[:PT, t_, :], AF.Square, [b.Xn[t_]], ["ssqf", "Rt0", "Rt1"], accum_out=stat[:PT, 6, t_:t_ + 1])
            rsqrt_small(stat[:PT, 7, 0:NT], stat[:PT, 6, 0:NT], 1.0 / D, ["ssqf"], ["rstdf"])
            for t_ in range(NT):
                stt(b.Xt[:PT, t_, :], b.Xt[:PT, t_, :], stat[:PT, 7, t_:t_ + 1], gfb[:PT, :], ALU.mult, ALU.mult,
                    [b.Xn[t_], "rstdf", "gfb"], [b.Xn[t_]])
            if b.sample:
                ld(ysm, b.Xt[:PT, 0, :], [], reads=b.Xn)
            else:
                jo = b.j - NPRE
                ld(y[jo * TB:(jo + 1) * TB, :].rearrange("(t p) d -> p t d", p=128), b.Xt[:], [], reads=b.Xn, us=9.0)
            yield

        def store_state(oconv, oh, T):
            for fc in range(4):
                ld(oconv[:, fc * 128:(fc + 1) * 128].rearrange("k p -> p k"), XB[:, fc, T:T + 3], [], reads=["XB%d" % fc], slow=True)
            ld(oh.rearrange("(c p) -> p c", p=128), Hst[:], [], reads=["Hst"], slow=True)

        def run(*gens):
            for g in gens:
                P.cur_tag = "%s" % getattr(g, "__name__", "?")
                for _ in g:
                    pass

        def chain(*gens):
            for g in gens:
                for _ in g:
                    yield

        def interleave(ga, gb, na=1, nb=1):
            da = db = False
            while not (da and db):
                for _ in range(na):
                    if not da:
                        try:
                            next(ga)
                        except StopIteration:
                            da = True
                for _ in range(nb):
                    if not db:
                        try:
                            next(gb)
                        except StopIteration:
                            db = True

        blks = [mkblk(j, "prefix" if j < NPRE else "full") for j in range(NBLK)]
        for j in range(NPRE):
            b = blks[j]
            st_load(b)
            run(st_norm(b, None, None), st_b1(b), st_b2(b))
            if deferred:
                o_, i_, n_ = deferred.pop(0)
                ld(o_, i_, [n_], eng="pool", us=12.0, issue=9.0, reads=["tok%d_0" % j])
        while deferred:
            o_, i_, n_ = deferred.pop(0)
            ld(o_, i_, [n_], eng="pool", us=12.0, issue=9.0)
        memset("pool", ex15[:, 0:1], 0.0, ["ring0", "ring1", "ring2", "yTa", "yTb0", "yTb1", "yTb2", "yTb3", "U", "VG", "YA", "Rt0", "Rt1"] + ["bs%d" % k for k in range(6)] + ["s1u%d" % k for k in range(16)],
               reads=["s1u%d" % k for k in range(16)])
        for kc in range(4):
            ts("dve", Wout[:, kc, :], Wout[:, kc, :], gnafm[:, kc:kc + 1], None, ALU.mult, None, ["Wout%d" % kc, "gnafm"], ["Wout%d" % kc])
        for j in range(NPRE, NBLK):
            b = blks[j]
            P.blk = j
            run(st_norm1s(b), st_a(b), st_bf(b))
            st_load(b)
            run(st_out(b), st_norm(b, g2b, "g2b"), st_up(b), st_down(b), st_final(b))
        store_state(oconv_p, oh_p, TB)
        bs = mkblk(0, "sample")
        st_load(bs)
        run(st_norm(bs, None, None), st_a(bs), st_bf(bs), st_out(bs), st_norm(bs, g2b, "g2b"),
            st_up(bs), st_down(bs), st_final(bs))
        store_state(oconv_s, oh_s, TS)
        sim = P.schedule()
        print("[sched] simulated time %.1f us, %d ops" % (sim, len(P.ops)))
        P.finalize()
    return nc


_NC_CACHE = {}


def _consts():
    ident = np.eye(128, dtype=np.float32)
    tril = np.tril(np.ones((128, 128), np.float32))
    E = np.zeros((64, 512), np.float32)
    for h in range(8):
        E[h, h * 64:(h + 1) * 64] = 1.0
        E[32 + h, h * 64:(h + 1) * 64] = 1.0
    return ident, tril, E


def kernel(x_prompt, x_sample, state_conv_b, state_h_b, norm1_g, w_in, ln_v_g, ln_v_b, w_s, b_s,
           conv_w, conv_b, w_a, b_a, w_x, b_x, lam, gn_a_g, gn_b_g, w_out, norm2_g, w_up, w_down, normf_g):
    f = lambda a: np.ascontiguousarray(np.asarray(a, dtype=np.float32))
    x_prompt = f(x_prompt)
    x_sample = f(x_sample)
    if "nc" not in _NC_CACHE:
        _NC_CACHE["nc"] = build_nc()
    nc = _NC_CACHE["nc"]
    ident, tril, E = _consts()
    shared = dict(
        c_ident=ident, c_tril=tril, c_E=E,
        norm1_g=f(norm1_g[0]), w_in=f(w_in[0]), ln_v_g=f(ln_v_g[0]), ln_v_b=f(ln_v_b[0]), w_s=f(w_s[0]), b_s=f(b_s[0]),
        conv_w=f(conv_w[0]), conv_b=f(conv_b[0]), w_a=f(w_a[0]), b_a=f(b_a[0]), w_x=f(w_x[0]), b_x=f(b_x[0]),
        lam=f(lam[0]), gn_a_g=f(gn_a_g[0]), gn_b_g=f(gn_b_g[0]), w_out=f(w_out[0]), norm2_g=f(norm2_g[0]),
        w_up=f(w_up[0]), w_down=f(w_down[0]), normf_g=f(normf_g),
    )
    in_maps = []
    for c in range(8):
        b, s = c // 4, c % 4
        npad = NPRE - 8 * s
        xs = np.zeros((NBLK * TB, D), np.float32)
        xs[npad * TB:] = x_prompt[b, :(s + 1) * NOWN * TB]
        keep = np.ones(32, np.float32)
        keep[:npad + 1] = 0.0
        first = np.zeros(32, np.float32)
        first[npad] = 1.0
        flags = np.concatenate([keep, first, 1.0 - first]).astype(np.float32)
        m = dict(shared)
        m.update(xs=xs, xsm=f(x_sample[c]), sconv=f(state_conv_b[0, c]), sh=f(state_h_b[0, c]), flags=flags)
        in_maps.append(m)
    res = run_bass_kernel_spmd(nc, in_maps, core_ids=list(range(8)))
    r = res.results
    y_prompt = np.stack([np.concatenate([r[b * 4 + s]["y"] for s in range(4)], axis=0) for b in range(2)])
    y_sample = np.stack([r[c]["ysm"] for c in range(8)])
    new_conv_p = np.stack([r[b * 4 + 3]["oconv_p"] for b in range(2)])[None]
    new_h_p = np.stack([r[b * 4 + 3]["oh_p"] for b in range(2)])[None]
    new_conv_s = np.stack([r[c]["oconv_s"] for c in range(8)])[None]
    new_h_s = np.stack([r[c]["oh_s"] for c in range(8)])[None]
    new_v_s = np.stack([r[c]["ov_s"] for c in range(8)])[None]
    return (y_prompt.astype(np.float32), y_sample.astype(np.float32), new_conv_p.astype(np.float32),
            new_h_p.astype(np.float32), new_conv_s.astype(np.float32), new_h_s.astype(np.float32),
            new_v_s.astype(np.float32))
```

```python
import numpy as np
import concourse.bass as bass
import concourse.mybir as mybir
from concourse.bass_utils import run_bass_kernel_spmd
from contextlib import ExitStack

F32 = mybir.dt.float32
BF16 = mybir.dt.bfloat16
AF = mybir.ActivationFunctionType
ALU = mybir.AluOpType

D = 1024
TB = 512
NPRE = 24
NOWN = 8
NBLK = NPRE + NOWN
TS = 32
EPS = 1e-6
SAME_DIST = 10 ** 9


class Prog:
    ENGS = ["pe", "act", "dve", "pool", "sp"]
    LAT = 1.0
    TABLE_US = 1.3

    def __init__(self, nc, es, rings):
        self.nc = nc
        self.ops = []
        self.last_w = {}
        self.readers = {}
        self.eng_ops = {e: [] for e in self.ENGS}
        self.R = rings
        self.csem = {e: es.enter_context(nc.semaphore("c_" + e)) for e in ["pe", "act", "dve", "pool"]}
        self.dsem = {e: [es.enter_context(nc.semaphore("d_%s%d" % (e, i))) for i in range(rings[e])]
                     for e in rings}

    def op(self, eng, fn, reads=(), writes=(), dma=False, dur=0.5, table=None, issue=0.1):
        i = len(self.ops)
        deps = set()
        for r in reads:
            if r in self.last_w:
                deps.add(self.last_w[r])
        for w in writes:
            if w in self.last_w:
                deps.add(self.last_w[w])
            deps.update(self.readers.get(w, ()))
        for r in reads:
            self.readers.setdefault(r, []).append(i)
        for w in writes:
            self.last_w[w] = i
            self.readers[w] = []
        deps.discard(i)
        self.ops.append(dict(eng=eng, fn=fn, deps=deps, dma=dma, signal=dma, dur=dur, table=table, issue=issue,
                             tag=(getattr(self, "blk", -1), getattr(self, "cur_tag", ""))))
        self.eng_ops[eng].append(i)
        return i

    def dma(self, eng, fn, reads=(), writes=(), dur=3.0, issue=0.1):
        return self.op(eng, fn, reads, writes, dma=True, dur=dur, issue=issue)

    def schedule(self):
        import heapq
        ops = self.ops
        n = len(ops)
        succ = [[] for _ in range(n)]
        indeg = [0] * n
        for i, o in enumerate(ops):
            indeg[i] = len(o["deps"])
            for d in o["deps"]:
                succ[d].append(i)
        bl = [0.0] * n
        for i in range(n - 1, -1, -1):
            m = 0.0
            for s_ in succ[i]:
                if bl[s_] > m:
                    m = bl[s_]
            bl[i] = m + ops[i]["dur"] + (self.LAT if succ[i] else 0.0)
        finish = [0.0] * n
        rt = [0.0] * n
        cand = {e: [] for e in self.ENGS}
        pend = {e: [] for e in self.ENGS}
        for i in range(n):
            if indeg[i] == 0:
                heapq.heappush(pend[ops[i]["eng"]], (0.0, i))
        free = {e: 0.0 for e in self.ENGS}
        order = {e: [] for e in self.ENGS}
        recent = {e: [] for e in self.ENGS}
        cur_table = [None]
        bus_free = 0.0
        left = n
        while left:
            best_e, best_t = None, None
            for e in self.ENGS:
                if cand[e]:
                    t = free[e]
                elif pend[e]:
                    t = max(free[e], pend[e][0][0])
                else:
                    continue
                if best_t is None or t < best_t:
                    best_e, best_t = e, t
            e, t = best_e, best_t
            while pend[e] and pend[e][0][0] <= t:
                r_, i_ = heapq.heappop(pend[e])
                heapq.heappush(cand[e], (-bl[i_], i_))
            pick = heapq.heappop(cand[e])
            if e != "pe" and cand[e]:
                rec = recent[e]

                def hazard(ix):
                    for d_ in ops[ix]["deps"]:
                        if d_ in rec:
                            return True
                    return False
                if hazard(pick[1]):
                    alt = []
                    found = None
                    while cand[e] and len(alt) < 16:
                        c = heapq.heappop(cand[e])
                        if not hazard(c[1]) and (-c[0]) > (-pick[0]) - 60.0:
                            found = c
                            break
                        alt.append(c)
                    for c in alt:
                        heapq.heappush(cand[e], c)
                    if found is not None:
                        heapq.heappush(cand[e], pick)
                        pick = found
            if e == "act" and len(cand[e]) > 0:
                def needs_switch(ix):
                    tb = ops[ix]["table"]
                    if tb is None:
                        return False
                    if tb == "tanh":
                        return cur_table[0] not in ("exp", "gelu")
                    return tb != cur_table[0]
                if needs_switch(pick[1]):
                    alt = []
                    found = None
                    while cand[e] and len(alt) < 24:
                        c = heapq.heappop(cand[e])
                        if not needs_switch(c[1]) and (-c[0]) > (-pick[0]) - 40.0:
                            found = c
                            break
                        alt.append(c)
                    for c in alt:
                        heapq.heappush(cand[e], c)
                    if found is not None:
                        heapq.heappush(cand[e], pick)
                        pick = found
            i = pick[1]
            o = ops[i]
            start = t
            if e != "pe":
                for d_ in o["deps"]:
                    if d_ in recent[e]:
                        start += 0.4
                        break
            if e == "act" and o["table"] is not None:
                tb = o["table"]
                if tb == "tanh":
                    if cur_table[0] not in ("exp", "gelu"):
                        start += self.TABLE_US
                        cur_table[0] = "exp"
                elif tb != cur_table[0]:
                    start += self.TABLE_US
                    cur_table[0] = tb
            if o["dma"]:
                free[e] = start + o["issue"]
                xs_ = max(start + o["issue"], bus_free)
                bus_free = xs_ + max(o["dur"] - 2.0, 0.1)
                finish[i] = xs_ + o["dur"]
            else:
                finish[i] = start + o["dur"]
                free[e] = finish[i]
            order[e].append(i)
            o["t0"] = start
            o["t1"] = finish[i]
            recent[e] = (recent[e] + [i])[-2:]
            left -= 1
            for s_ in succ[i]:
                lat = 0.0 if (ops[s_]["eng"] == e and not o["dma"]) else self.LAT
                v = finish[i] + lat
                if v > rt[s_]:
                    rt[s_] = v
                indeg[s_] -= 1
                if indeg[s_] == 0:
                    heapq.heappush(pend[ops[s_]["eng"]], (rt[s_], s_))
        self.eng_ops = order
        self.sim_time = max(finish) if n else 0.0
        return self.sim_time

    def finalize(self):
        ops = self.ops
        for e in self.ENGS:
            for k, i in enumerate(self.eng_ops[e]):
                ops[i]["lidx"] = k
        for o in ops:
            need = {}
            dd = []
            for d in o["deps"]:
                p = ops[d]
                if p["dma"]:
                    dd.append(d)
                    continue
                if p["eng"] == o["eng"]:
                    if o["eng"] == "pe":
                        continue
                    if o["lidx"] - p["lidx"] > SAME_DIST:
                        continue
                if p["eng"] not in need or ops[need[p["eng"]]]["lidx"] < p["lidx"]:
                    need[p["eng"]] = d
            o["cw"] = list(need.values())
            o["dw"] = dd
            for d in o["cw"]:
                ops[d]["signal"] = True
        for e in self.ENGS:
            c = 0
            n = 0
            for i in self.eng_ops[e]:
                o = ops[i]
                if o["dma"]:
                    o["slot"] = n % self.R[e]
                    o["val"] = 16 * (n // self.R[e] + 1)
                    o["sem"] = self.dsem[e][o["slot"]]
                    n += 1
                elif o["signal"]:
                    c += 1
                    o["val"] = c
                    o["sem"] = self.csem[e]

        def emit(e, eng):
            waited = {}

            def wait(sem, val):
                k = id(sem)
                if waited.get(k, 0) >= val:
                    return
                waited[k] = val
                eng.wait_ge(sem, val)

            last = {}
            for i in self.eng_ops[e]:
                o = ops[i]
                for d in o["cw"] + o["dw"]:
                    wait(ops[d]["sem"], ops[d]["val"])
                if o["dma"]:
                    if o["val"] > 16:
                        wait(o["sem"], o["val"] - 16)
                    ins = o["fn"](eng)
                    ins.then_inc(o["sem"], 16)
                    last[o["slot"]] = o
                else:
                    ins = o["fn"](eng)
                    if o["signal"]:
                        ins.then_inc(o["sem"], 1)
            for o in last.values():
                wait(o["sem"], o["val"])

        with self.nc.Block() as block:
            @block.tensor
            def _(eng):
                emit("pe", eng)

            @block.scalar
            def _(eng):
                emit("act", eng)

            @block.vector
            def _(eng):
                emit("dve", eng)

            @block.gpsimd
            def _(eng):
                emit("pool", eng)

            @block.sync
            def _(eng):
                emit("sp", eng)


def build_nc():
    nc = bass.Bass("TRN2", target_bir_lowering=False)

    def din(name, shape):
        return nc.dram_tensor(name, list(shape), F32, kind="ExternalInput").ap()

    def dout(name, shape):
        return nc.dram_tensor(name, list(shape), F32, kind="ExternalOutput").ap()

    xs = din("xs", [NBLK * TB, D])
    xsm = din("xsm", [TS, D])
    sconv = din("sconv", [3, 512])
    sh = din("sh", [512])
    flags = din("flags", [96])
    c_ident = din("c_ident", [128, 128])
    c_tril = din("c_tril", [128, 128])
    c_E = din("c_E", [64, 512])
    norm1_g = din("norm1_g", [D])
    w_in = din("w_in", [D, 2048])
    ln_v_g = din("ln_v_g", [512])
    ln_v_b = din("ln_v_b", [512])
    w_s = din("w_s", [8, 128, 128])
    b_s = din("b_s", [8, 128])
    conv_w = din("conv_w", [4, 512])
    conv_b = din("conv_b", [512])
    w_a = din("w_a", [8, 64, 64])
    b_a = din("b_a", [512])
    w_x = din("w_x", [8, 64, 64])
    b_x = din("b_x", [512])
    lam = din("lam", [512])
    gn_a_g = din("gn_a_g", [512])
    gn_b_g = din("gn_b_g", [512])
    w_out = din("w_out", [D, D])
    norm2_g = din("norm2_g", [D])
    w_up = din("w_up", [D, 4096])
    w_down = din("w_down", [4096, D])
    normf_g = din("normf_g", [D])

    y = dout("y", [NOWN * TB, D])
    ysm = dout("ysm", [TS, D])
    oconv_p = dout("oconv_p", [3, 512])
    oh_p = dout("oh_p", [512])
    oconv_s = dout("oconv_s", [3, 512])
    oh_s = dout("oh_s", [512])
    ov_s = dout("ov_s", [TS, 512])

    scr_up = nc.dram_tensor("scr_up", [D, 4096], BF16, kind="Internal").ap()
    scr_dn = nc.dram_tensor("scr_dn", [4096, D], BF16, kind="Internal").ap()

    with ExitStack() as es:
        def sb(name, shape, dt=F32):
            return es.enter_context(nc.sbuf_tensor(name, list(shape), dt))

        def pt(name, shape, dt=F32):
            return es.enter_context(nc.psum_tensor(name, list(shape), dt))

        P = Prog(nc, es, rings={"sp": 12, "pool": 40})

        Win = sb("Win", [128, 8, 2048], BF16)
        Wout = sb("Wout", [128, 8, 1024], BF16)
        WsT = sb("WsT", [128, 8, 128], BF16)
        identb = sb("identb", [128, 128], BF16)
        onesb = sb("onesb", [128, 128], BF16)
        Eb = sb("Eb", [64, 512], BF16)
        bsT = sb("bsT", [64, 128], BF16)
        g2b = sb("g2b", [128, D])
        gfb = sb("gfb", [128, D])
        lngb = sb("lngb", [128, 512])
        lnbb = sb("lnbb", [128, 512])
        flg = sb("flg", [128, 96])
        cwh = sb("cwh", [128, 4, 4])
        fmv = sb("fmv", [128, 8, 4])
        g1fm = sb("g1fm", [128, 8])
        gnafm = sb("gnafm", [128, 4])
        Hst = sb("Hst", [128, 4])
        h0t = sb("h0t", [128, 4])
        XB = sb("XB", [128, 4, TB + 3])
        X = [sb("X0", [128, 4, D]), sb("X1", [128, 4, D])]
        xnb0 = sb("xnb0", [128, D], BF16)
        xnb = [xnb0, xnb0]
        xnb1 = sb("xnb1", [128, D], BF16)
        xnT = [sb("xnT0", [128, 8, TB], BF16), sb("xnT1", [128, 8, TB], BF16)]
        yT = sb("yT", [128, 8, TB], BF16)
        U = sb("U", [128, 512])
        VG = sb("VG", [128, 512])
        YA = sb("YA", [128, 512])
        vbf = sb("vbf", [128, 512], BF16)
        yabf = sb("yabf", [128, 512], BF16)
        Rtt = sb("Rtt", [128, 2, 512], BF16)
        Rt = [Rtt[:, 0, :], Rtt[:, 1, :]]
        WA32 = sb("WA32", [128, 4, 128])
        WX32 = sb("WX32", [128, 4, 128])
        xcb = sb("xcb", [128, 4, 512], BF16)
        stat = sb("stat", [128, 16, 8])
        bnst = sb("bnst", [128, 6])
        arena = sb("arena", [128, 32, TB], BF16)
        ring = [sb("ring%d" % i, [128, 2048], BF16) for i in range(3)]
        bsc = sb("bsc", [128, 6, 512])
        stg = X[1][:, 0, :].rearrange("p (a b) -> p a b", a=8)
        STG = "X1t0"

        def AU(u):
            return arena[:, 2 * u:2 * u + 2, :].bitcast(F32).rearrange("p a b -> p (a b)")

        def AUn(u):
            return "ar%d" % u

        ex15 = Rtt[:].bitcast(F32).rearrange("p a b -> p (a b)")
        cst = sb("cst", [128, 2])

        arena_u = arena[:].bitcast(F32).rearrange("p (u a) b -> p u (a b)", a=2)

        def PHYS(k):
            return (k % 4) * 4 + (k // 4)

        class SC0:
            @staticmethod
            def u(k):
                return arena_u[:, PHYS(k), :]

            @staticmethod
            def n(k):
                return "ar%d" % PHYS(k)

            @staticmethod
            def group(k0, T):
                kind = k0 // 4
                v = arena[:].bitcast(F32).rearrange("p (f k a) b -> p f k (a b)", k=4, a=2)
                return v[:, :, kind, :T], ["ar%d" % PHYS(k0 + i) for i in range(4)]

            @staticmethod
            def sqrt_groups(T):
                return [SC0.group(4, T)]

            @staticmethod
            def half_bf(k):
                return arena[:, 2 * PHYS(k), :]

        class SC1:
            @staticmethod
            def u(k):
                if k < 6:
                    return ring[k // 2][:, 1024 * (k % 2):1024 * (k % 2 + 1)].bitcast(F32)
                if k < 8:
                    return bsc[:, k - 6, :]
                if k < 12:
                    kk = k - 8
                    return yT[:, 2 * kk:2 * kk + 2, :].bitcast(F32).rearrange("p a b -> p (a b)")
                return [U[:], VG[:], YA[:], ex15][k - 12]

            @staticmethod
            def n(k):
                return "s1u%d" % k

            @staticmethod
            def group(k0, T):
                return None

            @staticmethod
            def sqrt_groups(T):
                v = ring[2][:].bitcast(F32).rearrange("p (a b) -> p a b", a=2)
                return [(v[:, :, :T], ["s1u4", "s1u5"]), (bsc[:, 0:2, :T], ["s1u6", "s1u7"])]

        class SCF:
            @staticmethod
            def u(kind, fc):
                return bsc[:, 3 * (fc % 2) + kind, :]

            @staticmethod
            def n(kind, fc):
                return "bs%d" % (3 * (fc % 2) + kind)

            @staticmethod
            def pair(kind, T):
                v = bsc[:].rearrange("p (f k) n -> p f k n", k=3)
                return v[:, :, kind, :T], ["bs%d" % kind, "bs%d" % (3 + kind)]

        pstb = [pt("pst0", [128, 1024], BF16)]
        pbank = [pt("pb%d" % i, [128, 512], F32) for i in range(7)]
        state = {"pb": 0, "ring": 0, "mb": 0}
        MIX_BANKS = [0, 1, 2]
        MLP_BANKS = [3, 4, 5, 6]

        def nbank():
            k = MIX_BANKS[state["pb"] % 3]
            state["pb"] += 1
            return pbank[k], "pb%d" % k

        def mbank():
            k = MLP_BANKS[state["mb"] % 4]
            state["mb"] += 1
            return pbank[k], "pb%d" % k

        def npst():
            return pstb[0], "pst0"

        def nring():
            k = state["ring"]
            state["ring"] = (k + 1) % 3
            return ring[k], "ring%d" % k

        def fsz(ap):
            n = 1
            for d_ in ap.shape[1:]:
                n *= d_
            return n

        TABLE = {AF.Exp: "exp", AF.Tanh: "tanh", AF.Gelu_apprx_tanh: "gelu", AF.Sqrt: "sqrt", AF.Ln: "ln"}

        def act(out, in_, func, reads, writes, **kw):
            P.op("act", lambda e: e.activation(out=out, in_=in_, func=func, **kw), reads, writes,
                 dur=0.2 + fsz(in_) / 1000.0, table=TABLE.get(func))

        def edur(eng, n):
            if eng == "pool":
                return 0.2 + n / 600.0
            return 0.15 + n / 850.0

        def ts(eng, out, in0, s1, s2, op0, op1, reads, writes):
            if s2 is None:
                P.op(eng, lambda e: e.tensor_scalar(out=out, in0=in0, scalar1=s1, scalar2=None, op0=op0), reads, writes,
                     dur=edur(eng, fsz(out)))
            else:
                P.op(eng, lambda e: e.tensor_scalar(out=out, in0=in0, scalar1=s1, scalar2=s2, op0=op0, op1=op1), reads, writes,
                     dur=edur(eng, fsz(out)))

        def tt(eng, out, in0, in1, op, reads, writes):
            P.op(eng, lambda e: e.tensor_tensor(out=out, in0=in0, in1=in1, op=op), reads, writes,
                 dur=(0.2 + fsz(out) / 500.0) if eng == "pool" else edur(eng, fsz(out)))

        def stt(out, in0, scalar, in1, op0, op1, reads, writes):
            P.op("dve", lambda e: e.scalar_tensor_tensor(out=out, in0=in0, scalar=scalar, in1=in1, op0=op0, op1=op1), reads, writes,
                 dur=0.15 + fsz(out) / 850.0)

        def scan(out, d0, d1, init, reads, writes):
            P.op("dve", lambda e: e.tensor_tensor_scan(out=out, data0=d0, data1=d1, initial=init, op0=ALU.mult, op1=ALU.add),
                 reads, writes, dur=0.2 + fsz(out) / 480.0)

        def recip(out, in_, reads, writes):
            P.op("dve", lambda e: e.reciprocal(out=out, in_=in_), reads, writes, dur=0.15 + fsz(out) / 850.0)

        def cp(eng, out, in_, reads, writes):
            if eng == "act":
                act(out, in_, AF.Copy, reads, writes)
            else:
                n = fsz(out)
                d_ = (0.2 + n / 280.0) if eng == "pool" else (0.15 + n / 850.0)
                P.op(eng, lambda e: e.tensor_copy(out=out, in_=in_), reads, writes, dur=d_)

        def mm(out, lhsT, rhs, start, stop, reads, writes, f32=False):
            P.op("pe", lambda e: e.matmul(out, lhsT=lhsT, rhs=rhs, start=start, stop=stop), reads, writes,
                 dur=(0.02 + fsz(rhs) / 2300.0) * (4.0 if f32 else 1.0))

        def tr(out, in_, ident, reads, writes):
            P.op("pe", lambda e: e.transpose(out, in_, ident), reads, writes, dur=0.1)

        def ld(out, in_, writes, reads=(), eng="sp", slow=False, us=3.0, issue=0.1):
            if slow:
                P.dma(eng, lambda e: e.dma_start(out=out, in_=in_, allow_slow_non_contiguous=True), reads, writes, dur=us, issue=issue)
            else:
                P.dma(eng, lambda e: e.dma_start(out=out, in_=in_), reads, writes, dur=us, issue=issue)

        def memset(eng, ap, val, writes, reads=()):
            P.op(eng, lambda e: e.memset(ap, val), reads, writes, dur=0.2 + fsz(ap) / 1000.0)

        def ppow(out, in0, col, reads, writes):
            ex = cst[:out.shape[0], col:col + 1].to_broadcast(list(out.shape))
            P.op("pool", lambda e: e.tensor_tensor(out=out, in0=in0, in1=ex, op=ALU.pow), list(reads) + ["cst"], writes,
                 dur=0.3 + fsz(out) * 0.16)

        def rsqrt_small(dst, src, scale, reads, writes):
            ts("pool", dst, src, scale, EPS, ALU.mult, ALU.add, reads, writes)
            ppow(dst, dst, 0, writes, writes)

        memset("pool", cst[:, 0:1], -0.5, ["cst"])
        memset("pool", cst[:, 1:2], 0.5, ["cst"], reads=["cst"])
        memset("pool", onesb[:], 1.0, ["onesb"])
        memset("pool", bsT[:], 0.0, ["bsT"])
        memset("pool", Hst[:], 0.0, ["Hst"])
        memset("pool", XB[:], 0.0, ["XB0", "XB1", "XB2", "XB3"])
        ld(g1fm[:], norm1_g.rearrange("(c p) -> p c", p=128), ["g1fm"], slow=True)
        stgW = arena[:].bitcast(F32).rearrange("p (s a) b -> p s (a b)", s=4)
        for kc in range(8):
            sl = kc % 4
            nm_ = ["ar%d" % (sl * 4 + i) for i in range(4)]
            ld(stgW[:, sl, :], w_in[kc * 128:(kc + 1) * 128, :], nm_, us=4.5)
            if kc % 2 == 0:
                ts("dve", Win[:, kc, :], stgW[:, sl, :], g1fm[:, kc:kc + 1], None, ALU.mult, None, nm_ + ["g1fm"], ["Win%d" % kc])
            else:
                act(Win[:, kc, :], stgW[:, sl, :], AF.Copy, nm_ + ["g1fm"], ["Win%d" % kc], scale=g1fm[:, kc:kc + 1])
        deferred = []
        for kc in range(8):
            deferred.append((Wout[:, kc, :], w_out[kc * 128:(kc + 1) * 128, :], "Wout%d" % kc))
        for i in range(8):
            deferred.append((scr_up[i * 128:(i + 1) * 128, :], w_up[i * 128:(i + 1) * 128, :], "scr_up"))
        for i in range(8):
            deferred.append((scr_dn[i * 512:(i + 1) * 512, :], w_down[i * 512:(i + 1) * 512, :], "scr_dn"))

        ld(g2b[:], norm2_g.partition_broadcast(128), ["g2b"])
        ld(gfb[:], normf_g.partition_broadcast(128), ["gfb"])
        ld(lngb[:], ln_v_g.partition_broadcast(128), ["lngb"])
        ld(lnbb[:], ln_v_b.partition_broadcast(128), ["lnbb"])
        ld(flg[:], flags.partition_broadcast(128), ["flg"])
        ld(gnafm[:], gn_a_g.rearrange("(c p) -> p c", p=128), ["gnafm"], slow=True)
        for kc in range(8):
            pass
        WIN = ["Win%d" % kc for kc in range(8)]
        WOUT = ["Wout%d" % kc for kc in range(8)]

        ld(stg[:, 0, :], c_ident, [STG])
        cp("dve", identb[:], stg[:, 0, :], [STG], ["identb"])
        ld(YA[0:64, :], c_E, ["YA"])
        cp("dve", Eb[:], YA[0:64, :], ["YA"], ["Eb"])
        ld(VG[0:8, 0:128], b_s, ["VG"])
        ld(VG[32:40, 128:256], b_s, ["VG"])
        cp("dve", bsT[0:8, :], VG[0:8, 0:128], ["VG"], ["bsT"])
        cp("dve", yabf[32:40, 0:128], VG[32:40, 128:256], ["VG"], ["yabf"])
        cp("dve", VG[32:40, 256:384], yabf[32:40, 0:128], ["yabf"], ["VG"])
        tt("dve", VG[32:40, 128:256], VG[32:40, 128:256], VG[32:40, 256:384], ALU.subtract, ["VG"], ["VG"])
        cp("dve", bsT[32:40, :], VG[32:40, 128:256], ["VG"], ["bsT"])

        ld(stg[:], w_s.rearrange("h t s -> t h s"), [STG], reads=[STG])
        ld(U[:, 0:128], c_tril, ["U"])
        for h in range(8):
            tt("dve", YA[:, 0:128], stg[:, h, :], U[:, 0:128], ALU.mult, [STG, "U"], ["YA"])
            cp("dve", vbf[:, 0:128], YA[:, 0:128], ["YA"], ["vbf"])
            pstt, pn = npst()
            tr(pstt[:, 0:128], vbf[:, 0:128], identb[:], ["vbf", "identb"], [pn])
            cp("dve", WsT[:, h, :], pstt[:, 0:128], [pn], ["WsT"])

        for (wsrc, w32, nm) in ((w_a, WA32, "WA"), (w_x, WX32, "WX")):
            memset("pool", w32[:], 0.0, [nm + "32"])
            for h in range(8):
                po = (h % 2) * 64
                ld(w32[po:po + 64, h // 2, po:po + 64], wsrc[h], [nm + "32"], reads=[nm + "32"])

        def fml(dst, src, nm):
            ld(dst, src.rearrange("(c p) -> p c", p=128), [nm], slow=True)
        for k in range(4):
            fml(cwh[:, k, :], conv_w[k], "cwh")
        fml(fmv[:, 0, :], conv_b, "fmv0")
        fml(fmv[:, 1, :], b_a, "fmv1")
        fml(fmv[:, 2, :], b_x, "fmv2")
        fml(fmv[:, 3, :], lam, "fmv3")
        fml(fmv[:, 4, :], gn_b_g, "fmv4")
        ts("dve", cwh[:].rearrange("p a b -> p (a b)"), cwh[:].rearrange("p a b -> p (a b)"), 0.5, None, ALU.mult, None, ["cwh"], ["cwh"])
        for i in range(3):
            ts("dve", fmv[:, i, :], fmv[:, i, :], 0.5, None, ALU.mult, None, ["fmv%d" % i], ["fmv%d" % i])
        act(fmv[:, 7, :], fmv[:, 3, :], AF.Exp, ["fmv3"], ["fmv7"], scale=-1.0)
        act(fmv[:, 7, :], fmv[:, 7, :], AF.Ln, ["fmv7"], ["fmv7"], bias=1.0)
        ts("dve", fmv[:, 5, :], fmv[:, 7, :], -4.0, None, ALU.mult, None, ["fmv7"], ["fmv5"])
        ts("dve", fmv[:, 6, :], fmv[:, 7, :], -8.0, None, ALU.mult, None, ["fmv7"], ["fmv6"])

        class Blk:
            pass

        def mkblk(j, kind):
            b = Blk()
            b.j = j
            b.sample = kind == "sample"
            b.full = kind != "prefix"
            b.T = TS if b.sample else TB
            b.PT = TS if b.sample else 128
            b.NT = 1 if b.sample else 4
            b.slot = 0 if b.sample else j % 2
            b.Xt = X[b.slot]
            b.Xn = ["X%dt%d" % (b.slot, t_) for t_ in range(b.NT)]
            b.xT = xnT[b.slot]
            b.xTn = "xnT%d" % b.slot
            b.sc = SC1 if (kind == "prefix" and j % 2 == 1) else SC0
            return b

        def st_load(b):
            if b.sample:
                ld(b.Xt[:b.PT, 0, :], xsm, b.Xn)
            else:
                ld(b.Xt[:], xs[b.j * TB:(b.j + 1) * TB, :].rearrange("(t p) d -> p t d", p=128), b.Xn, us=9.0)

        def st_norm(b, gb, gname):
            PT, NT = b.PT, b.NT
            for t_ in range(NT):
                act(xnb0[:PT, :], b.Xt[:PT, t_, :], AF.Square, [b.Xn[t_]], ["ssq", "xnb0"], accum_out=stat[:PT, 0, t_:t_ + 1])
            rsqrt_small(stat[:PT, 1, 0:NT], stat[:PT, 0, 0:NT], 1.0 / D, ["ssq"], ["rstd"])
            yield

            def scale(t_):
                xb_, xbn = xnb[t_ % 2], "xnb0"
                if gb is None:
                    ts("dve", xb_[:PT, :], b.Xt[:PT, t_, :], stat[:PT, 1, t_:t_ + 1], None, ALU.mult, None,
                       [b.Xn[t_], "rstd"], [xbn])
                else:
                    stt(xb_[:PT, :], b.Xt[:PT, t_, :], stat[:PT, 1, t_:t_ + 1], gb[:PT, :], ALU.mult, ALU.mult,
                        [b.Xn[t_], "rstd", gname], [xbn])

            def trans(t_):
                xb_, xbn = xnb[t_ % 2], "xnb0"
                if gb is not None:
                    mb_, pn = mbank()
                    pstt = mb_[:].bitcast(BF16)
                else:
                    pstt, pn = npst()
                for kc in range(8):
                    tr(pstt[:, kc * PT:(kc + 1) * PT], xb_[:PT, kc * 128:(kc + 1) * 128], identb[:PT, :PT],
                       [xbn, "identb"], [pn])
                eng = "act" if t_ % 2 == 0 else "dve"
                cp(eng, b.xT[:, :, t_ * 128:t_ * 128 + PT], pstt[:, 0:8 * PT].rearrange("p (k t) -> p k t", k=8),
                   [pn], [b.xTn])

            scale(0)
            yield
            for t_ in range(1, NT):
                trans(t_ - 1)
                scale(t_)
                yield
            trans(NT - 1)
            yield

        xstg = xcb[:].bitcast(F32).rearrange("p a b -> p (a b)")
        XSTG = ["xcb0", "xcb1", "xcb2", "xcb3"]

        def st_norm1s(b):
            for t_ in range(4):
                sq, rs = "ssq1_%d" % t_, "rstd1_%d" % t_
                ld(xstg, xs[b.j * TB + t_ * 128:b.j * TB + (t_ + 1) * 128, :], XSTG, us=3.5)
                act(xnb1[:, :], xstg, AF.Square, XSTG, [sq, "xnb1"], accum_out=stat[:, 9, t_:t_ + 1])
                ts("pool", stat[:, 10, t_:t_ + 1], stat[:, 9, t_:t_ + 1], 1.0 / D, EPS, ALU.mult, ALU.add, [sq], [rs])
                ppow(stat[:, 10, t_:t_ + 1], stat[:, 10, t_:t_ + 1], 0, [rs], [rs])
                ts("dve", xnb1[:, :], xstg, stat[:, 10, t_:t_ + 1], None, ALU.mult, None, XSTG + [rs], ["xnb1"])
                pstt, pn = npst()
                for kc in range(8):
                    tr(pstt[:, kc * 128:(kc + 1) * 128], xnb1[:, kc * 128:(kc + 1) * 128], identb[:, :],
                       ["xnb1", "identb"], [pn])
                eng = "act" if t_ % 2 == 0 else "dve"
                cp(eng, b.xT[:, :, t_ * 128:(t_ + 1) * 128], pstt[:, 0:1024].rearrange("p (k t) -> p k t", k=8),
                   [pn], [b.xTn])
                yield

        def st_b1(b):
            T, j = b.T, b.j
            AU, AUn = b.sc.u, b.sc.n
            merged = b.sc.group(4, T) is not None
            if b.sample:
                for fc in range(4):
                    ld(XB[:, fc, 0:3], sconv[:, fc * 128:(fc + 1) * 128].rearrange("k p -> p k"), ["XB%d" % fc], slow=True)
            pbs = []
            for fc in range(4):
                pb, pbn = nbank()
                for kc in range(8):
                    mm(pb[:, :T], Win[:, kc, 1024 + fc * 128:1024 + (fc + 1) * 128], b.xT[:, kc, :T], kc == 0, kc == 7,
                       [WIN[kc], b.xTn], [pbn])
                if not b.sample and j > 0:
                    cp("act", XB[:, fc, 0:3], XB[:, fc, TB:TB + 3], ["XB%d" % fc], ["XB%d" % fc])
                cp("act", XB[:, fc, 3:3 + T], pb[:, :T], [pbn], ["XB%d" % fc])
                yield
            for fc in range(4):
                TMP, TMPn = AU(12 + fc), AUn(12 + fc)
                ts("pool", TMP[:, :T], XB[:, fc, 0:T], cwh[:, 0, fc:fc + 1], fmv[:, 0, fc:fc + 1], ALU.mult, ALU.add,
                   ["XB%d" % fc, "cwh", "fmv0"], [TMPn])
            yield
            for k in range(1, 4):
                for fc in range(4):
                    TMP, TMPn = AU(12 + fc), AUn(12 + fc)
                    stt(TMP[:, :T], XB[:, fc, k:k + T], cwh[:, k, fc:fc + 1], TMP[:, :T], ALU.mult, ALU.add,
                        ["XB%d" % fc, "cwh", TMPn], [TMPn])
                yield
            f32g = not b.full
            if not f32g:
                for fc in range(4):
                    cp("pool", xcb[:, fc, :T], AU(12 + fc)[:, :T], [AUn(12 + fc)], ["xcb%d" % fc])
            yield
            for fc in range(4):
                A_, T1_, V_ = AU(fc), AU(4 + fc), AU(8 + fc)
                An, T1n, Vn = AUn(fc), AUn(4 + fc), AUn(8 + fc)
                pr, prn = nbank()
                if f32g:
                    mm(pr[:, :T], WA32[:, fc, :], AU(12 + fc)[:, :T], True, True, ["WA32", AUn(12 + fc)], [prn], f32=True)
                else:
                    mm(pr[:, :T], WA[:, fc, :], xcb[:, fc, :T], True, True, ["WA", "xcb%d" % fc], [prn])
                act(T1_[:, :T], pr[:, :T], AF.Tanh, [prn, "fmv1"], [T1n], bias=fmv[:, 1, fc:fc + 1])
                pi, pin = nbank()
                if f32g:
                    mm(pi[:, :T], WX32[:, fc, :], AU(12 + fc)[:, :T], True, True, ["WX32", AUn(12 + fc)], [pin], f32=True)
                else:
                    mm(pi[:, :T], WX[:, fc, :], xcb[:, fc, :T], True, True, ["WX", "xcb%d" % fc], [pin])
                act(V_[:, :T], pi[:, :T], AF.Tanh, [pin, "fmv2"], [Vn], bias=fmv[:, 2, fc:fc + 1])
                yield
            for fc in range(4):
                A_, T1_, V_ = AU(fc), AU(4 + fc), AU(8 + fc)
                An, T1n, Vn = AUn(fc), AUn(4 + fc), AUn(8 + fc)
                TMP, TMPn = AU(12 + fc), AUn(12 + fc)
                act(A_[:, :T], T1_[:, :T], AF.Exp, [T1n, "fmv5"], [An], scale=fmv[:, 5, fc:fc + 1], bias=fmv[:, 5, fc:fc + 1])
                act(T1_[:, :T], T1_[:, :T], AF.Exp, [T1n, "fmv6"], [T1n], scale=fmv[:, 6, fc:fc + 1], bias=fmv[:, 6, fc:fc + 1])
                if merged:
                    continue
                ts("pool", T1_[:, :T], T1_[:, :T], -1.0, 1.0, ALU.mult, ALU.add, [T1n], [T1n])
                stt(V_[:, :T], V_[:, :T], 1.0, TMP[:, :T], ALU.add, ALU.mult, [Vn, TMPn], [Vn])
                if (not b.sample) and j % 8 == 0:
                    ts("pool", T1_[:, 0:1], T1_[:, 0:1], flg[:, 64 + j:64 + j + 1], flg[:, 32 + j:32 + j + 1],
                       ALU.mult, ALU.add, [T1n, "flg"], [T1n])
                yield
            if merged:
                gT1, nT1 = b.sc.group(4, T)
                gV, nV = b.sc.group(8, T)
                gTM, nTM = b.sc.group(12, T)
                ts("pool", gT1, gT1, -1.0, 1.0, ALU.mult, ALU.add, nT1, nT1)
                stt(gV, gV, 1.0, gTM, ALU.add, ALU.mult, nV + nTM, nV)
                if (not b.sample) and j % 8 == 0:
                    for fc in range(4):
                        T1_, T1n = AU(4 + fc), AUn(4 + fc)
                        ts("pool", T1_[:, 0:1], T1_[:, 0:1], flg[:, 64 + j:64 + j + 1], flg[:, 32 + j:32 + j + 1],
                           ALU.mult, ALU.add, [T1n, "flg"], [T1n])
                yield

        def st_b2(b):
            T, j = b.T, b.j
            AU, AUn = b.sc.u, b.sc.n
            if b.sample:
                ld(h0t[:], sh.rearrange("(c p) -> p c", p=128), ["h0t"], slow=True)
            else:
                ts("dve", h0t[:], Hst[:], flg[:, j:j + 1], None, ALU.mult, None, ["Hst", "flg"], ["h0t"])
            for (v_, n_) in b.sc.sqrt_groups(T):
                act(v_, v_, AF.Sqrt, n_, n_)
            yield
            if b.sc.group(4, T) is not None:
                gT1, nT1 = b.sc.group(4, T)
                gV, nV = b.sc.group(8, T)
                tt("dve", gV, gV, gT1, ALU.mult, nV + nT1, nV)
            else:
                for fc in range(4):
                    A_, T1_, V_ = AU(fc), AU(4 + fc), AU(8 + fc)
                    An, T1n, Vn = AUn(fc), AUn(4 + fc), AUn(8 + fc)
                    tt("dve", V_[:, :T], V_[:, :T], T1_[:, :T], ALU.mult, [Vn, T1n], [Vn])
            yield
            for fc in range(4):
                A_, T1_, V_ = AU(fc), AU(4 + fc), AU(8 + fc)
                An, T1n, Vn = AUn(fc), AUn(4 + fc), AUn(8 + fc)
                scan(T1_[:, :T], A_[:, :T], V_[:, :T], h0t[:, fc:fc + 1], [An, Vn, "h0t"], [T1n])
                cp("act", Hst[:, fc:fc + 1], T1_[:, T - 1:T], [T1n], ["Hst", "tok%d_%d" % (b.j, int(b.sample))])
                yield
            if not b.full:
                return
            for fc in range(4):
                A_, T1_, V_ = AU(fc), AU(4 + fc), AU(8 + fc)
                An, T1n, Vn = AUn(fc), AUn(4 + fc), AUn(8 + fc)
                TMP, TMPn = AU(12 + fc), AUn(12 + fc)
                pg, pgn = nbank()
                for kc in range(8):
                    mm(pg[:, :T], Win[:, kc, 1536 + fc * 128:1536 + (fc + 1) * 128], b.xT[:, kc, :T], kc == 0, kc == 7,
                       [WIN[kc], b.xTn], [pgn])
                act(TMP[:, :T], pg[:, :T], AF.Gelu_apprx_tanh, [pgn], [TMPn])
                stt(A_[:, :T], T1_[:, :T], fmv[:, 4, fc:fc + 1], TMP[:, :T], ALU.mult, ALU.mult, [T1n, "fmv4", TMPn], [An])
                ysq = SC0.half_bf(8 + fc)
                act(ysq[:, :T], A_[:, :T], AF.Square, [An], [Vn])
            yield
            pb, pbn = nbank()
            for fc in range(4):
                mm(pb[:, :T], onesb[:], SC0.half_bf(8 + fc)[:, :T], fc == 0, fc == 3, ["onesb", AUn(8 + fc)], [pbn])
            RB, RBn = AU(12), AUn(12)
            ts("dve", RB[:, :T], pb[:, :T], 1.0 / 512, EPS, ALU.mult, ALU.add, [pbn], [RBn])
            act(RB[:, :T], RB[:, :T], AF.Sqrt, [RBn], [RBn])
            recip(RB[:, :T], RB[:, :T], [RBn], [RBn])
            for fc in range(4):
                tt("pool", yT[:, 4 + fc, :T], AU(fc)[:, :T], RB[:, :T], ALU.mult, [AUn(fc), RBn], ["yTb"])
            yield

        def st_bf(b):
            T, j, PT, NT = b.T, b.j, b.PT, b.NT
            if b.sample:
                ld(h0t[:], sh.rearrange("(c p) -> p c", p=128), ["h0t"], slow=True)
                for fc in range(4):
                    ld(XB[:, fc, 0:3], sconv[:, fc * 128:(fc + 1) * 128].rearrange("k p -> p k"), ["XB%d" % fc], slow=True)
            else:
                ts("dve", h0t[:], Hst[:], flg[:, j:j + 1], None, ALU.mult, None, ["Hst", "flg"], ["h0t"])
            for half in range(2):
                fcs = (2 * half, 2 * half + 1)
                for fc in fcs:
                    pb, pbn = nbank()
                    for kc in range(8):
                        mm(pb[:, :T], Win[:, kc, 1024 + fc * 128:1024 + (fc + 1) * 128], b.xT[:, kc, :T], kc == 0, kc == 7,
                           [WIN[kc], b.xTn], [pbn])
                    if not b.sample and j > 0:
                        cp("act", XB[:, fc, 0:3], XB[:, fc, TB:TB + 3], ["XB%d" % fc], ["XB%d" % fc])
                    cp("act", XB[:, fc, 3:3 + T], pb[:, :T], [pbn], ["XB%d" % fc])
                yield
                for fc in fcs:
                    ts("pool", SCF.u(0, fc)[:, :T], XB[:, fc, 0:T], cwh[:, 0, fc:fc + 1], fmv[:, 0, fc:fc + 1], ALU.mult, ALU.add,
                       ["XB%d" % fc, "cwh", "fmv0"], [SCF.n(0, fc)])
                for k in range(1, 4):
                    for fc in fcs:
                        stt(SCF.u(0, fc)[:, :T], XB[:, fc, k:k + T], cwh[:, k, fc:fc + 1], SCF.u(0, fc)[:, :T], ALU.mult, ALU.add,
                            ["XB%d" % fc, "cwh", SCF.n(0, fc)], [SCF.n(0, fc)])
                yield
                for fc in fcs:
                    TM, TMn = SCF.u(0, fc), SCF.n(0, fc)
                    T1_, T1n = SCF.u(1, fc), SCF.n(1, fc)
                    V_, Vn = SCF.u(2, fc), SCF.n(2, fc)
                    pr, prn = nbank()
                    mm(pr[:, :T], WA32[:, fc, :], TM[:, :T], True, True, ["WA32", TMn], [prn], f32=True)
                    act(T1_[:, :T], pr[:, :T], AF.Tanh, [prn, "fmv1"], [T1n], bias=fmv[:, 1, fc:fc + 1])
                    pi, pin = nbank()
                    mm(pi[:, :T], WX32[:, fc, :], TM[:, :T], True, True, ["WX32", TMn], [pin], f32=True)
                    act(V_[:, :T], pi[:, :T], AF.Tanh, [pin, "fmv2"], [Vn], bias=fmv[:, 2, fc:fc + 1])
                    stt(V_[:, :T], V_[:, :T], 1.0, TM[:, :T], ALU.add, ALU.mult, [Vn, TMn], [Vn])
                    yield
                for fc in fcs:
                    A_, An = SCF.u(0, fc), SCF.n(0, fc)
                    T1_, T1n = SCF.u(1, fc), SCF.n(1, fc)
                    act(A_[:, :T], T1_[:, :T], AF.Exp, [T1n, "fmv5"], [An], scale=fmv[:, 5, fc:fc + 1], bias=fmv[:, 5, fc:fc + 1])
                    act(T1_[:, :T], T1_[:, :T], AF.Exp, [T1n, "fmv6"], [T1n], scale=fmv[:, 6, fc:fc + 1], bias=fmv[:, 6, fc:fc + 1])
                gT1, nT1 = SCF.pair(1, T)
                gV, nV = SCF.pair(2, T)
                ts("pool", gT1, gT1, -1.0, 1.0, ALU.mult, ALU.add, nT1, nT1)
                if (not b.sample) and j % 8 == 0:
                    for fc in fcs:
                        T1_, T1n = SCF.u(1, fc), SCF.n(1, fc)
                        ts("pool", T1_[:, 0:1], T1_[:, 0:1], flg[:, 64 + j:64 + j + 1], flg[:, 32 + j:32 + j + 1],
                           ALU.mult, ALU.add, [T1n, "flg"], [T1n])
                act(gT1, gT1, AF.Sqrt, nT1, nT1)
                tt("dve", gV, gV, gT1, ALU.mult, nV + nT1, nV)
                yield
                for fc in fcs:
                    A_, An = SCF.u(0, fc), SCF.n(0, fc)
                    T1_, T1n = SCF.u(1, fc), SCF.n(1, fc)
                    V_, Vn = SCF.u(2, fc), SCF.n(2, fc)
                    scan(T1_[:, :T], A_[:, :T], V_[:, :T], h0t[:, fc:fc + 1], [An, Vn, "h0t"], [T1n])
                    cp("act", Hst[:, fc:fc + 1], T1_[:, T - 1:T], [T1n], ["Hst", "tok%d_%d" % (b.j, int(b.sample))])
                    pg, pgn = nbank()
                    for kc in range(8):
                        mm(pg[:, :T], Win[:, kc, 1536 + fc * 128:1536 + (fc + 1) * 128], b.xT[:, kc, :T], kc == 0, kc == 7,
                           [WIN[kc], b.xTn], [pgn])
                    act(A_[:, :T], pg[:, :T], AF.Gelu_apprx_tanh, [pgn, An], [An])
                    stt(yT[:, 4 + fc, :T], T1_[:, :T], fmv[:, 4, fc:fc + 1], A_[:, :T], ALU.mult, ALU.mult, [T1n, "fmv4", An], ["yTb%d" % fc])
                    act(xcb[:, fc, :T], yT[:, 4 + fc, :T], AF.Square, ["yTb%d" % fc], ["xcb%d" % fc])
                    yield
            pk, pkn = nbank()
            for t_ in range(NT):
                tok = slice(t_ * 128, t_ * 128 + PT)
                for fc in range(4):
                    mm(pk[:PT, t_:t_ + 1], xcb[:, fc, tok], onesb[:, 0:1], fc == 0, fc == 3, ["xcb%d" % fc, "onesb"], [pkn])
            ts("dve", stat[:PT, 8, 0:NT], pk[:PT, 0:NT], 1.0 / 512, EPS, ALU.mult, ALU.add, [pkn], ["rsb"])
            ppow(stat[:PT, 8, 0:NT], stat[:PT, 8, 0:NT], 0, ["rsb"], ["rsb"])
            yield

        def st_a(b):
            PT, NT = b.PT, b.NT
            for t_ in range(NT):
                tok = slice(t_ * 128, t_ * 128 + PT)
                pu, pun = nbank()
                for kc in range(8):
                    mm(pu[:PT, :], b.xT[:, kc, tok], Win[:, kc, 0:512], kc == 0, kc == 7, [WIN[kc], b.xTn], [pun])
                pv, pvn = nbank()
                for kc in range(8):
                    mm(pv[:PT, :], b.xT[:, kc, tok], Win[:, kc, 512:1024], kc == 0, kc == 7, [WIN[kc], b.xTn], [pvn])
                act(U[:PT, :], pu[:PT, :], AF.Gelu_apprx_tanh, [pun], ["U"])
                act(VG[:PT, :], pv[:PT, :], AF.Gelu_apprx_tanh, [pvn], ["VG"])
                yield
                P.op("dve", lambda e: e.bn_stats(out=bnst[:PT, :], in_=VG[:PT, :]), ["VG"], ["bnst"], dur=0.75)
                P.op("dve", lambda e: e.bn_aggr(out=stat[:PT, 2, 0:2], in_=bnst[:PT, :]), ["bnst"], ["mv"], dur=0.2)
                rsqrt_small(stat[:PT, 3, 0:1], stat[:PT, 2, 1:2], 1.0, ["mv"], ["lnr"])
                stt(stat[:PT, 3, 1:2], stat[:PT, 2, 0:1], -1.0, stat[:PT, 3, 0:1], ALU.mult, ALU.mult, ["mv", "lnr"], ["lnb2"])
                act(VG[:PT, :], VG[:PT, :], AF.Identity, ["VG", "lnr", "lnb2"], ["VG"], scale=stat[:PT, 3, 0:1], bias=stat[:PT, 3, 1:2])
                tt("dve", VG[:PT, :], VG[:PT, :], lngb[:PT, :], ALU.mult, ["VG", "lngb"], ["VG"])
                if b.sample:
                    tt("pool", VG[:PT, :], VG[:PT, :], lnbb[:PT, :], ALU.add, ["VG", "lnbb"], ["VG"])
                    ld(ov_s, VG[:PT, :], [], reads=["VG"])
                    cp("pool", vbf[:PT, :], VG[:PT, :], ["VG"], ["vbf"])
                else:
                    tt("pool", vbf[:PT, :], VG[:PT, :], lnbb[:PT, :], ALU.add, ["VG", "lnbb"], ["vbf"])
                yield
                pm, pmn = nbank()
                mm(pm[:PT, :], bsT[:, :PT], Eb[:, :], True, False, ["bsT", "Eb"], [pmn])
                for h in range(8):
                    hs = slice(h * 64, (h + 1) * 64)
                    mm(pm[:PT, hs], WsT[:PT, h, :PT], vbf[:PT, hs], False, h == 7, ["WsT", "vbf"], [pmn])
                tt("dve", YA[:PT, :], U[:PT, :], pm[:PT, :], ALU.mult, ["U", pmn], ["YA"])
                act(yabf[:PT, :], YA[:PT, :], AF.Square, ["YA"], ["ssa", "yabf"], accum_out=stat[:PT, 4, 0:1])
                rsqrt_small(stat[:PT, 5, 0:1], stat[:PT, 4, 0:1], 1.0 / 512, ["ssa"], ["rsa"])
                ts("dve", yabf[:PT, :], YA[:PT, :], stat[:PT, 5, 0:1], None, ALU.mult, None, ["YA", "rsa"], ["yabf"])
                yield
                pstt, pn = npst()
                for kc in range(4):
                    tr(pstt[:, kc * PT:(kc + 1) * PT], yabf[:PT, kc * 128:(kc + 1) * 128], identb[:PT, :PT],
                       ["yabf", "identb"], [pn])
                cp("act", yT[:, 0:4, t_ * 128:t_ * 128 + PT], pstt[:, 0:4 * PT].rearrange("p (k t) -> p k t", k=4),
                   [pn], ["yTa"])
                yield

        def st_out(b):
            PT, NT = b.PT, b.NT
            for t_ in range(NT):
                tok = slice(t_ * 128, t_ * 128 + PT)
                for nh in range(2):
                    cols = slice(nh * 512, (nh + 1) * 512)
                    pa, pan = nbank()
                    for kc in range(4):
                        mm(pa[:PT, :], yT[:, kc, tok], Wout[:, kc, cols], kc == 0, kc == 3, [WOUT[kc], "yTa"], [pan])
                    pb_, pbn_ = nbank()
                    for kc in range(4, 8):
                        mm(pb_[:PT, :], yT[:, kc, tok], Wout[:, kc, cols], kc == 4, kc == 7, [WOUT[kc], "yTb%d" % (kc - 4)], [pbn_])
                    tt("dve", b.Xt[:PT, t_, cols], b.Xt[:PT, t_, cols], pa[:PT, :], ALU.add, [b.Xn[t_], pan], [b.Xn[t_]])
                    stt(b.Xt[:PT, t_, cols], pb_[:PT, :], stat[:PT, 8, t_:t_ + 1], b.Xt[:PT, t_, cols], ALU.mult, ALU.add,
                        [b.Xn[t_], pbn_, "rsb"], [b.Xn[t_]])
                yield

        def st_up(b):
            T = b.T
            for ug in range(16):
                rg, rgn = nring()
                rgv = rg[:].rearrange("p (k n) -> p k n", k=8)
                ld(rgv, scr_up[:, ug * 256:(ug + 1) * 256].rearrange("(k p) n -> p k n", p=128), [rgn], reads=["scr_up"], us=3.5)
                for f in range(2):
                    ffc = ug * 2 + f
                    ph, phn = mbank()
                    for kc in range(8):
                        mm(ph[:, :T], rgv[:, kc, f * 128:(f + 1) * 128], b.xT[:, kc, :T], kc == 0, kc == 7, [rgn, b.xTn], [phn])
                    R_, Rn = Rt[ffc % 2], "Rt%d" % (ffc % 2)
                    act(R_[:, :T], ph[:, :T], AF.Relu, [phn], [Rn])
                    tt("dve", arena[:, ffc, :T], ph[:, :T], R_[:, :T], ALU.mult, [phn, Rn], [AUn(ffc // 2)])
                    yield

        def st_down(b):
            PT, NT = b.PT, b.NT
            for nh in range(2):
                accs = [mbank() for _ in range(NT)]
                if NT < 4:
                    state["mb"] += 4 - NT
                for dg in range(8):
                    rg, rgn = nring()
                    rgv = rg[:].rearrange("p (f n) -> p f n", f=4)
                    ld(rgv, scr_dn[dg * 512:(dg + 1) * 512, nh * 512:(nh + 1) * 512].rearrange("(f p) n -> p f n", p=128),
                       [rgn], reads=["scr_dn"], us=3.5)
                    for t_ in range(NT):
                        tok = slice(t_ * 128, t_ * 128 + PT)
                        for f in range(4):
                            ffc = dg * 4 + f
                            mm(accs[t_][0][:PT, :], arena[:, ffc, tok], rgv[:, f, :], dg == 0 and f == 0, dg == 7 and f == 3,
                               [AUn(ffc // 2), rgn], [accs[t_][1]])
                        if t_ % 2 == 1:
                            yield
                    if NT == 1:
                        yield
                for t_ in range(NT):
                    tt("dve", b.Xt[:PT, t_, nh * 512:(nh + 1) * 512], b.Xt[:PT, t_, nh * 512:(nh + 1) * 512], accs[t_][0][:PT, :],
                       ALU.add, [b.Xn[t_], accs[t_][1]], [b.Xn[t_]])
                yield

        def st_final(b):
            PT, NT = b.PT, b.NT
            for t_ in range(NT):
                act(Rtt[:PT].rearrange("p a b -> p (a b)"), b.Xt[:PT, t_, :], AF.Square, [b.Xn[t_]], ["ssqf", "Rt0", "Rt1"], accum_out=stat[:PT, 6, t_:t_ + 1])
            rsqrt_small(stat[:PT, 7, 0:NT], stat[:PT, 6, 0:NT], 1.0 / D, ["ssqf"], ["rstdf"])
            for t_ in range(NT):
                stt(b.Xt[:PT, t_, :], b.Xt[:PT, t_, :], stat[:PT, 7, t_:t_ + 1], gfb[:PT, :], ALU.mult, ALU.mult,
                    [b.Xn[t_], "rstdf", "gfb"], [b.Xn[t_]])
            if b.sample:
                ld(ysm, b.Xt[:PT, 0, :], [], reads=b.Xn)
            else:
                jo = b.j - NPRE
                ld(y[jo * TB:(jo + 1) * TB, :].rearrange("(t p) d -> p t d", p=128), b.Xt[:], [], reads=b.Xn, us=9.0)
            yield

        def store_state(oconv, oh, T):
            for fc in range(4):
                ld(oconv[:, fc * 128:(fc + 1) * 128].rearrange("k p -> p k"), XB[:, fc, T:T + 3], [], reads=["XB%d" % fc], slow=True)
            ld(oh.rearrange("(c p) -> p c", p=128), Hst[:], [], reads=["Hst"], slow=True)

        def run(*gens):
            for g in gens:
                P.cur_tag = "%s" % getattr(g, "__name__", "?")
                for _ in g:
                    pass

        def chain(*gens):
            for g in gens:
                for _ in g:
                    yield

        def interleave(ga, gb, na=1, nb=1):
            da = db = False
            while not (da and db):
                for _ in range(na):
                    if not da:
                        try:
                            next(ga)
                        except StopIteration:
                            da = True
                for _ in range(nb):
                    if not db:
                        try:
                            next(gb)
                        except StopIteration:
                            db = True

        blks = [mkblk(j, "prefix" if j < NPRE else "full") for j in range(NBLK)]
        for j in range(NPRE):
            b = blks[j]
            st_load(b)
            run(st_norm(b, None, None), st_b1(b), st_b2(b))
            if deferred:
                o_, i_, n_ = deferred.pop(0)
                ld(o_, i_, [n_], eng="pool", us=12.0, issue=9.0, reads=["tok%d_0" % j])
        while deferred:
            o_, i_, n_ = deferred.pop(0)
            ld(o_, i_, [n_], eng="pool", us=12.0, issue=9.0)
        memset("pool", ex15[:, 0:1], 0.0, ["ring0", "ring1", "ring2", "yTa", "yTb0", "yTb1", "yTb2", "yTb3", "U", "VG", "YA", "Rt0", "Rt1"] + ["bs%d" % k for k in range(6)] + ["s1u%d" % k for k in range(16)],
               reads=["s1u%d" % k for k in range(16)])
        for kc in range(4):
            ts("dve", Wout[:, kc, :], Wout[:, kc, :], gnafm[:, kc:kc + 1], None, ALU.mult, None, ["Wout%d" % kc, "gnafm"], ["Wout%d" % kc])
        for j in range(NPRE, NBLK):
            b = blks[j]
            P.blk = j
            run(st_norm1s(b), st_a(b), st_bf(b))
            st_load(b)
            run(st_out(b), st_norm(b, g2b, "g2b"), st_up(b), st_down(b), st_final(b))
        store_state(oconv_p, oh_p, TB)
        bs = mkblk(0, "sample")
        st_load(bs)
        run(st_norm(bs, None, None), st_a(bs), st_bf(bs), st_out(bs), st_norm(bs, g2b, "g2b"),
            st_up(bs), st_down(bs), st_final(bs))
        store_state(oconv_s, oh_s, TS)
        sim = P.schedule()
        print("[sched] simulated time %.1f us, %d ops" % (sim, len(P.ops)))
        P.finalize()
    return nc


_NC_CACHE = {}


def _consts():
    ident = np.eye(128, dtype=np.float32)
    tril = np.tril(np.ones((128, 128), np.float32))
    E = np.zeros((64, 512), np.float32)
    for h in range(8):
        E[h, h * 64:(h + 1) * 64] = 1.0
        E[32 + h, h * 64:(h + 1) * 64] = 1.0
    return ident, tril, E


def kernel(x_prompt, x_sample, state_conv_b, state_h_b, norm1_g, w_in, ln_v_g, ln_v_b, w_s, b_s,
           conv_w, conv_b, w_a, b_a, w_x, b_x, lam, gn_a_g, gn_b_g, w_out, norm2_g, w_up, w_down, normf_g):
    f = lambda a: np.ascontiguousarray(np.asarray(a, dtype=np.float32))
    x_prompt = f(x_prompt)
    x_sample = f(x_sample)
    if "nc" not in _NC_CACHE:
        _NC_CACHE["nc"] = build_nc()
    nc = _NC_CACHE["nc"]
    ident, tril, E = _consts()
    shared = dict(
        c_ident=ident, c_tril=tril, c_E=E,
        norm1_g=f(norm1_g[0]), w_in=f(w_in[0]), ln_v_g=f(ln_v_g[0]), ln_v_b=f(ln_v_b[0]), w_s=f(w_s[0]), b_s=f(b_s[0]),
        conv_w=f(conv_w[0]), conv_b=f(conv_b[0]), w_a=f(w_a[0]), b_a=f(b_a[0]), w_x=f(w_x[0]), b_x=f(b_x[0]),
        lam=f(lam[0]), gn_a_g=f(gn_a_g[0]), gn_b_g=f(gn_b_g[0]), w_out=f(w_out[0]), norm2_g=f(norm2_g[0]),
        w_up=f(w_up[0]), w_down=f(w_down[0]), normf_g=f(normf_g),
    )
    in_maps = []
    for c in range(8):
        b, s = c // 4, c % 4
        npad = NPRE - 8 * s
        xs = np.zeros((NBLK * TB, D), np.float32)
        xs[npad * TB:] = x_prompt[b, :(s + 1) * NOWN * TB]
        keep = np.ones(32, np.float32)
        keep[:npad + 1] = 0.0
        first = np.zeros(32, np.float32)
        first[npad] = 1.0
        flags = np.concatenate([keep, first, 1.0 - first]).astype(np.float32)
        m = dict(shared)
        m.update(xs=xs, xsm=f(x_sample[c]), sconv=f(state_conv_b[0, c]), sh=f(state_h_b[0, c]), flags=flags)
        in_maps.append(m)
    res = run_bass_kernel_spmd(nc, in_maps, core_ids=list(range(8)))
    r = res.results
    y_prompt = np.stack([np.concatenate([r[b * 4 + s]["y"] for s in range(4)], axis=0) for b in range(2)])
    y_sample = np.stack([r[c]["ysm"] for c in range(8)])
    new_conv_p = np.stack([r[b * 4 + 3]["oconv_p"] for b in range(2)])[None]
    new_h_p = np.stack([r[b * 4 + 3]["oh_p"] for b in range(2)])[None]
    new_conv_s = np.stack([r[c]["oconv_s"] for c in range(8)])[None]
    new_h_s = np.stack([r[c]["oh_s"] for c in range(8)])[None]
    new_v_s = np.stack([r[c]["ov_s"] for c in range(8)])[None]
    return (y_prompt.astype(np.float32), y_sample.astype(np.float32), new_conv_p.astype(np.float32),
            new_h_p.astype(np.float32), new_conv_s.astype(np.float32), new_h_s.astype(np.float32),
            new_v_s.astype(np.float32))
```
